# Optimizing a Trainium2 kernel written in Bass

```python
import math
import jax, jax.numpy as jnp
from jax import lax
import numpy as np

D_MODEL = 1024
BATCH = 4
SEQ = 4096
DEPTH = 4

GRID_W = 64
CTX_LEN = 256
CHUNK = 128
N_MIXERS = 3
EPS = 1e-6
D_FF = -(-8 * D_MODEL // (3 * 256)) * 256

SSD_INNER = 2 * D_MODEL
SSD_HEAD_DIM = 64
SSD_HEADS = SSD_INNER // SSD_HEAD_DIM
SSD_GROUPS = 4
SSD_HPG = SSD_HEADS // SSD_GROUPS
SSD_STATE = 128
SSD_CONV = 5
SSD_CONV_DIM = SSD_INNER + 2 * SSD_GROUPS * SSD_STATE
SSD_IN_DIM = SSD_INNER + SSD_CONV_DIM + 2 * SSD_HEADS

GMLP_WIDTH = 2 * D_MODEL
GMLP_GROUPS = 8
GMLP_GROUP_DIM = GMLP_WIDTH // GMLP_GROUPS

RET_HEADS = D_MODEL // 256
RET_QK = D_MODEL // RET_HEADS
RET_V = 2 * D_MODEL // RET_HEADS
RET_IN_DIM = 2 * D_MODEL + 2 * (2 * D_MODEL)
ROPE_BASE = 10000.0

N_A = (DEPTH + 2) // 3
N_B = (DEPTH + 1) // 3
N_C = DEPTH // 3

kernel_name = "hybrid_ssd_gmlp_retention_dit"


def rmsnorm(x, g):
    xf = x.astype(jnp.float32)
    y = xf * lax.rsqrt(jnp.mean(xf * xf, axis=-1, keepdims=True) + EPS)
    return (y * g).astype(x.dtype)


def modulate(h, shift, scale):
    return h * (1.0 + scale) + shift


def swiglu(h, w1, w3, w2):
    return (jax.nn.silu(h @ w1) * (h @ w3)) @ w2


def dwconv(x, w, b):
    k = w.shape[0]
    y = lax.conv_general_dilated(x, w[:, None, :].astype(x.dtype), window_strides=(1,),
                                 padding=[(k // 2, k // 2)],
                                 dimension_numbers=("NWC", "WIO", "NWC"),
                                 feature_group_count=x.shape[-1])
    return y + b


def rope(t, pos):
    half = t.shape[-1] // 2
    inv = ROPE_BASE ** (-jnp.arange(half, dtype=jnp.float32) / half)
    ang = pos.astype(jnp.float32)[:, None] * inv
    cos = jnp.cos(ang)[:, None, :]
    sin = jnp.sin(ang)[:, None, :]
    t1 = t[..., :half].astype(jnp.float32)
    t2 = t[..., half:].astype(jnp.float32)
    return jnp.concatenate([t1 * cos - t2 * sin, t2 * cos + t1 * sin], axis=-1).astype(t.dtype)


def axial_rope(t, row, col):
    d = t.shape[-1] // 2
    return jnp.concatenate([rope(t[..., :d], row), rope(t[..., d:], col)], axis=-1)


def chunked_decay_scan(q, k, v, log_a, s0):
    bsz, length = q.shape[:2]
    nc = length // CHUNK

    def to_chunks(t):
        return jnp.swapaxes(t.reshape(bsz, nc, CHUNK, *t.shape[2:]), 0, 1)

    mask = jnp.tril(jnp.ones((CHUNK, CHUNK), bool))[None, :, :, None, None]

    def step(s, inp):
        qi, ki, vi, ai = inp
        cum = jnp.cumsum(ai.astype(jnp.float32), axis=1)
        seg = cum[:, :, None] - cum[:, None]
        decay = jnp.where(mask, jnp.exp(jnp.where(mask, seg, 0.0)), 0.0)
        scores = jnp.einsum("bign,bjgn->bijg", qi, ki)
        y = jnp.einsum("bijgr,bjgrp->bigrp", scores[..., None] * decay, vi)
        y = y + jnp.einsum("bign,bgrnp->bigrp", qi, s) * jnp.exp(cum)[..., None]
        tail = jnp.exp(cum[:, -1:] - cum)
        s = s * jnp.exp(cum[:, -1])[..., None, None] + jnp.einsum(
            "bjgn,bjgrp->bgrnp", ki, vi * tail[..., None])
        return s, y

    s, ys = lax.scan(step, s0, tuple(to_chunks(t) for t in (q, k, v, log_a)))
    y = jnp.swapaxes(ys, 0, 1).reshape(bsz, length, *v.shape[2:])
    return y, s


def bidir_prefix(ctx_in, lat_in, s0):
    qc, kc, vcf, acf, vcb, acb = ctx_in
    ql, kl, vlf, alf, vlb, alb = lat_in
    rev = lambda t: jnp.flip(t, axis=1)
    yc_f, sc_f = chunked_decay_scan(qc, kc, vcf, acf, s0)
    yc_b, sc_b = chunked_decay_scan(rev(qc), rev(kc), rev(vcb), rev(acb), s0)
    yl_f, _ = chunked_decay_scan(ql, kl, vlf, alf, sc_f)
    yl_b, _ = chunked_decay_scan(rev(ql), rev(kl), rev(vlb), rev(alb), sc_b)
    return yc_f + rev(yc_b), yl_f + rev(yl_b)


def ssd_mixer(a_ctx, a_lat, w_in, conv_w, conv_b, a_log_f, a_log_b, dt_bias_f, dt_bias_b,
              d_skip, norm_g, w_out, need_ctx):
    A_f = -jnp.exp(a_log_f.astype(jnp.float32)).reshape(SSD_GROUPS, SSD_HPG)
    A_b = -jnp.exp(a_log_b.astype(jnp.float32)).reshape(SSD_GROUPS, SSD_HPG)

    def project(h):
        bsz, length = h.shape[:2]
        z, xbc, dt = jnp.split(h @ w_in, [SSD_INNER, SSD_INNER + SSD_CONV_DIM], axis=-1)
        xbc = jax.nn.silu(dwconv(xbc, conv_w, conv_b))
        xs, bm, cm = jnp.split(xbc, [SSD_INNER, SSD_INNER + SSD_GROUPS * SSD_STATE], axis=-1)
        xs = xs.reshape(bsz, length, SSD_GROUPS, SSD_HPG, SSD_HEAD_DIM)
        bm = bm.reshape(bsz, length, SSD_GROUPS, SSD_STATE)
        cm = cm.reshape(bsz, length, SSD_GROUPS, SSD_STATE)
        dt_f = jax.nn.softplus(dt[..., :SSD_HEADS] + dt_bias_f).reshape(bsz, length, SSD_GROUPS, SSD_HPG)
        dt_b = jax.nn.softplus(dt[..., SSD_HEADS:] + dt_bias_b).reshape(bsz, length, SSD_GROUPS, SSD_HPG)
        scan_in = (cm, bm, xs * dt_f[..., None], dt_f * A_f, xs * dt_b[..., None], dt_b * A_b)
        return z, xs, scan_in

    def finish(y, z, xs):
        y = y + d_skip.reshape(SSD_GROUPS, SSD_HPG)[:, :, None] * xs
        y = y.reshape(*z.shape[:2], SSD_INNER) * jax.nn.silu(z)
        return rmsnorm(y, norm_g) @ w_out

    z_c, xs_c, in_c = project(a_ctx)
    z_l, xs_l, in_l = project(a_lat)
    s0 = jnp.zeros((a_lat.shape[0], SSD_GROUPS, SSD_HPG, SSD_STATE, SSD_HEAD_DIM), jnp.float32)
    y_c, y_l = bidir_prefix(in_c, in_l, s0)
    o_ctx = finish(y_c, z_c, xs_c) if need_ctx else None
    return o_ctx, finish(y_l, z_l, xs_l)


def gmlp_mixer(a_ctx, a_lat, w_in, norm_g, w_s, b_s, w_out, need_ctx):
    def mix(h):
        bsz, length = h.shape[:2]
        u, v = jnp.split(jax.nn.gelu(h @ w_in), 2, axis=-1)
        v = rmsnorm(v, norm_g).reshape(bsz, length // CHUNK, CHUNK, GMLP_GROUPS, GMLP_GROUP_DIM)
        v = jnp.einsum("gij,bcjgd->bcigd", w_s, v) + b_s.T[:, :, None]
        return (u * v.reshape(bsz, length, GMLP_WIDTH)) @ w_out

    o_ctx = mix(a_ctx) if need_ctx else None
    return o_ctx, mix(a_lat)


def retention_mixer(a_ctx, a_lat, w_in, decay_f, decay_b, w_out, row, col, need_ctx):
    lg_f = -jnp.exp(decay_f.astype(jnp.float32))
    lg_b = -jnp.exp(decay_b.astype(jnp.float32))

    def project(h, rotate):
        bsz, length = h.shape[:2]
        q, k, v, g = jnp.split(h @ w_in, [D_MODEL, 2 * D_MODEL, 4 * D_MODEL], axis=-1)
        q = q.reshape(bsz, length, RET_HEADS, RET_QK)
        k = k.reshape(bsz, length, RET_HEADS, RET_QK) * (RET_QK ** -0.5)
        if rotate:
            q = axial_rope(q, row, col)
            k = axial_rope(k, row, col)
        v = v.reshape(bsz, length, RET_HEADS, 1, RET_V)
        af = jnp.broadcast_to(lg_f[:, None], (bsz, length, RET_HEADS, 1))
        ab = jnp.broadcast_to(lg_b[:, None], (bsz, length, RET_HEADS, 1))
        return g, (q, k, v, af, v, ab)

    def finish(y, g):
        bsz, length = g.shape[:2]
        y = y.reshape(bsz, length, RET_HEADS, RET_V).astype(jnp.float32)
        y = y * lax.rsqrt(jnp.mean(y * y, axis=-1, keepdims=True) + EPS)
        y = y.reshape(bsz, length, 2 * D_MODEL).astype(g.dtype)
        return (jax.nn.silu(g) * y) @ w_out

    g_c, in_c = project(a_ctx, False)
    g_l, in_l = project(a_lat, True)
    s0 = jnp.zeros((a_lat.shape[0], RET_HEADS, 1, RET_QK, RET_V), jnp.float32)
    y_c, y_l = bidir_prefix(in_c, in_l, s0)
    o_ctx = finish(y_c, g_c) if need_ctx else None
    return o_ctx, finish(y_l, g_l)


def setup_inputs(seed: int = 0) -> dict:
    key = jax.random.key(seed)
    ks = iter(jax.random.split(key, 40))
    f32 = jnp.float32

    def nrm(shape, scale):
        return jax.random.normal(next(ks), shape, f32) * scale

    def gain(shape):
        return 1.0 + nrm(shape, 0.02)

    def dt_bias():
        dt = jnp.exp(jax.random.uniform(next(ks), (N_A, SSD_HEADS), f32, math.log(1e-3), math.log(1e-1)))
        return dt + jnp.log(-jnp.expm1(-dt))

    inv = D_MODEL ** -0.5
    x = nrm((BATCH, SEQ, D_MODEL), 1.0)
    c = nrm((BATCH, D_MODEL), 1.0)
    ctx = nrm((BATCH, CTX_LEN, D_MODEL), 1.0)
    c_ctx = nrm((D_MODEL,), 1.0)
    mod_w = nrm((DEPTH, D_MODEL, 6 * D_MODEL), 0.5 * inv)
    mod_b = nrm((DEPTH, 6 * D_MODEL), 0.02)
    norm1_g = gain((DEPTH, D_MODEL))
    norm2_g = gain((DEPTH, D_MODEL))
    ffn_w1 = nrm((DEPTH, D_MODEL, D_FF), inv)
    ffn_w3 = nrm((DEPTH, D_MODEL, D_FF), inv)
    ffn_w2 = nrm((DEPTH, D_FF, D_MODEL), D_FF ** -0.5)
    ssd_w_in = nrm((N_A, D_MODEL, SSD_IN_DIM), inv)
    ssd_conv_w = nrm((N_A, SSD_CONV, SSD_CONV_DIM), SSD_CONV ** -0.5)
    ssd_conv_b = nrm((N_A, SSD_CONV_DIM), 0.02)
    ssd_a_log_f = jnp.log(jax.random.uniform(next(ks), (N_A, SSD_HEADS), f32, 1.0, 16.0))
    ssd_a_log_b = jnp.log(jax.random.uniform(next(ks), (N_A, SSD_HEADS), f32, 1.0, 16.0))
    ssd_dt_bias_f = dt_bias()
    ssd_dt_bias_b = dt_bias()
    ssd_d = gain((N_A, SSD_HEADS))
    ssd_norm_g = gain((N_A, SSD_INNER))
    ssd_w_out = nrm((N_A, SSD_INNER, D_MODEL), SSD_INNER ** -0.5)
    gmlp_w_in = nrm((N_B, D_MODEL, 2 * GMLP_WIDTH), inv)
    gmlp_norm_g = gain((N_B, GMLP_WIDTH))
    gmlp_w_s = nrm((N_B, GMLP_GROUPS, CHUNK, CHUNK), CHUNK ** -0.5)
    gmlp_b_s = gain((N_B, GMLP_GROUPS, CHUNK))
    gmlp_w_out = nrm((N_B, GMLP_WIDTH, D_MODEL), GMLP_WIDTH ** -0.5)
    ret_w_in = nrm((N_C, D_MODEL, RET_IN_DIM), inv)
    base = jnp.log(-jnp.log1p(-(2.0 ** (-5.0 - jnp.arange(RET_HEADS, dtype=f32)))))
    ret_decay_f = base + nrm((N_C, RET_HEADS), 0.05)
    ret_decay_b = base + nrm((N_C, RET_HEADS), 0.05)
    ret_w_out = nrm((N_C, 2 * D_MODEL, D_MODEL), (2 * D_MODEL) ** -0.5)
    final_g = gain((D_MODEL,))
    return {"x": x, "c": c, "ctx": ctx, "c_ctx": c_ctx, "mod_w": mod_w, "mod_b": mod_b,
            "norm1_g": norm1_g, "norm2_g": norm2_g, "ffn_w1": ffn_w1, "ffn_w3": ffn_w3, "ffn_w2": ffn_w2,
            "ssd_w_in": ssd_w_in, "ssd_conv_w": ssd_conv_w, "ssd_conv_b": ssd_conv_b,
            "ssd_a_log_f": ssd_a_log_f, "ssd_a_log_b": ssd_a_log_b,
            "ssd_dt_bias_f": ssd_dt_bias_f, "ssd_dt_bias_b": ssd_dt_bias_b, "ssd_d": ssd_d,
            "ssd_norm_g": ssd_norm_g, "ssd_w_out": ssd_w_out,
            "gmlp_w_in": gmlp_w_in, "gmlp_norm_g": gmlp_norm_g, "gmlp_w_s": gmlp_w_s,
            "gmlp_b_s": gmlp_b_s, "gmlp_w_out": gmlp_w_out,
            "ret_w_in": ret_w_in, "ret_decay_f": ret_decay_f, "ret_decay_b": ret_decay_b,
            "ret_w_out": ret_w_out, "final_g": final_g}


def reference(x, c, ctx, c_ctx, mod_w, mod_b, norm1_g, norm2_g, ffn_w1, ffn_w3, ffn_w2,
              ssd_w_in, ssd_conv_w, ssd_conv_b, ssd_a_log_f, ssd_a_log_b, ssd_dt_bias_f,
              ssd_dt_bias_b, ssd_d, ssd_norm_g, ssd_w_out,
              gmlp_w_in, gmlp_norm_g, gmlp_w_s, gmlp_b_s, gmlp_w_out,
              ret_w_in, ret_decay_f, ret_decay_b, ret_w_out, final_g):
    n_tok = x.shape[1]
    ROWS = n_tok // GRID_W
    pos = jnp.arange(ROWS * GRID_W)
    row = pos // GRID_W
    col = pos % GRID_W
    c_act = jax.nn.silu(c)
    c_ctx_act = jax.nn.silu(c_ctx)
    h_ctx = ctx
    for i in range(DEPTH):
        last = i == DEPTH - 1
        kind, j = i % N_MIXERS, i // N_MIXERS
        m_lat = (c_act @ mod_w[i] + mod_b[i])[:, None, :]
        m_ctx = c_ctx_act @ mod_w[i] + mod_b[i]
        sh1, sc1, g1, sh2, sc2, g2 = jnp.split(m_lat, 6, axis=-1)
        csh1, csc1, cg1, csh2, csc2, cg2 = jnp.split(m_ctx, 6, axis=-1)
        a_lat = modulate(rmsnorm(x, norm1_g[i]), sh1, sc1)
        a_ctx = modulate(rmsnorm(h_ctx, norm1_g[i]), csh1, csc1)
        if kind == 0:
            o_ctx, o_lat = ssd_mixer(a_ctx, a_lat, ssd_w_in[j], ssd_conv_w[j], ssd_conv_b[j],
                                     ssd_a_log_f[j], ssd_a_log_b[j], ssd_dt_bias_f[j],
                                     ssd_dt_bias_b[j], ssd_d[j], ssd_norm_g[j], ssd_w_out[j],
                                     not last)
        elif kind == 1:
            o_ctx, o_lat = gmlp_mixer(a_ctx, a_lat, gmlp_w_in[j], gmlp_norm_g[j], gmlp_w_s[j],
                                      gmlp_b_s[j], gmlp_w_out[j], not last)
        else:
            o_ctx, o_lat = retention_mixer(a_ctx, a_lat, ret_w_in[j], ret_decay_f[j],
                                           ret_decay_b[j], ret_w_out[j], row, col, not last)
        x = x + g1 * o_lat
        x = x + g2 * swiglu(modulate(rmsnorm(x, norm2_g[i]), sh2, sc2), ffn_w1[i], ffn_w3[i], ffn_w2[i])
        if not last:
            h_ctx = h_ctx + cg1 * o_ctx
            h_ctx = h_ctx + cg2 * swiglu(modulate(rmsnorm(h_ctx, norm2_g[i]), csh2, csc2),
                                         ffn_w1[i], ffn_w3[i], ffn_w2[i])
    return rmsnorm(x, final_g)
```

```python
import math
from contextlib import ExitStack
import numpy as np
import concourse.bass as bass
import concourse.mybir as mybir
from concourse.alu_op_type import AluOpType as ALU
from concourse.bass_utils import run_bass_kernel_spmd

AF = mybir.ActivationFunctionType
F32 = mybir.dt.float32
BF16 = mybir.dt.bfloat16
I32 = mybir.dt.int32

D = 1024
SEQ = 4096
CTX = 256
NCH = (SEQ + CTX) // 128
NT = NCH * 128
DFF = 2816
EPS = 1e-6
NEG = -1.0e30
N_CORES = 4
SEM_ROT = 6000


class Buf:
    __slots__ = ("name", "w", "rs", "grp")

    def __init__(self, name="", grp=None):
        self.name = name
        self.w = None
        self.rs = []
        self.grp = grp


class BufMap:
    def __init__(self, name):
        self.name = name
        self.d = {}

    def __getitem__(self, k):
        if k not in self.d:
            self.d[k] = Buf(f"{self.name}{k}", grp=self.name)
        return self.d[k]

    def all(self):
        return list(self.d.values())


class Op:
    __slots__ = ("eng", "fn", "deps", "signal", "is_dma", "key", "sem", "cnt", "ninc", "phase")


class Prog:
    ENGS = ("pe", "dve", "act", "pool", "sp")

    def __init__(self, nc):
        self.nc = nc
        self.ops = {e: [] for e in self.ENGS}
        self.slot_cnt = []
        self.slot_gen = []
        self.gen_final = {}
        self.key2slot = {}
        self.touched = {}
        self.phase = 0

    def op(self, eng, fn, reads=(), writes=(), dma=False, key=None, ninc=1, extra=()):
        o = Op()
        o.eng, o.fn, o.is_dma, o.signal, o.ninc = eng, fn, dma, dma, ninc
        o.sem = None
        o.cnt = None
        o.key = None
        o.phase = self.phase
        deps = list(extra)
        for b in reads:
            if b.w is not None:
                deps.append(b.w)
            self.touched[id(b)] = b
        for b in writes:
            if b.w is not None:
                deps.append(b.w)
            deps.extend(b.rs)
            self.touched[id(b)] = b
        dd, seen = [], set()
        for d in deps:
            if id(d) in seen or d is o:
                continue
            seen.add(id(d))
            if (not d.is_dma) and d.eng == eng and (not dma) and eng == "pe":
                continue
            req = None
            if d.is_dma:
                si, gen = d.key
                req = self.slot_cnt[si] if self.slot_gen[si] == gen else self.gen_final[(si, gen)]
            dd.append((d, req))
        o.deps = dd
        if dma:
            if key is None:
                key = (writes[0].grp or id(writes[0])) if writes else id(o)
            if key not in self.key2slot:
                self.key2slot[key] = len(self.key2slot)
                if len(self.key2slot) > len(self.slot_cnt):
                    self.slot_cnt.append(0)
                    self.slot_gen.append(0)
            si = self.key2slot[key]
            self.slot_cnt[si] += 16 * ninc
            o.key = (si, self.slot_gen[si])
            o.cnt = self.slot_cnt[si]
        for b in reads:
            b.rs.append(o)
        for b in writes:
            b.w = o
            b.rs = []
        self.ops[eng].append(o)
        return o

    def barrier(self):
        fin, seen = [], set()
        for b in self.touched.values():
            for o in ([b.w] if b.w is not None else []) + list(b.rs):
                if id(o) not in seen:
                    seen.add(id(o))
                    fin.append(o)
        for e in self.ENGS:
            self.op(e, None, extra=fin)
        self.touched = {}
        self.key2slot = {}
        self.phase += 1
        for i in range(len(self.slot_cnt)):
            if self.slot_cnt[i] > SEM_ROT:
                self.gen_final[(i, self.slot_gen[i])] = self.slot_cnt[i]
                self.slot_cnt[i] = 0
                self.slot_gen[i] += 1

    def emit(self):
        nc = self.nc
        for e in self.ENGS:
            for o in self.ops[e]:
                for d, _ in o.deps:
                    d.signal = True
        dsem = {}
        nes = 0
        for e in self.ENGS:
            c = 0
            cur = None
            ph = -1
            for o in self.ops[e]:
                if o.is_dma:
                    if o.key not in dsem:
                        dsem[o.key] = nc.alloc_semaphore(name=f"d{len(dsem)}")
                    o.sem = dsem[o.key]
                elif o.signal:
                    if cur is None or (o.phase != ph and c > SEM_ROT):
                        cur = nc.alloc_semaphore(name=f"s_{e}{nes}")
                        nes += 1
                        c = 0
                    ph = o.phase
                    c += 1
                    o.sem = cur
                    o.cnt = c
        print("semaphores used:", nes, "+", len(dsem), flush=True)

        def run(e, engobj):
            waited = {}
            for o in self.ops[e]:
                for d, req in o.deps:
                    if d.sem is None:
                        continue
                    cnt = d.cnt if req is None else req
                    k = id(d.sem)
                    if waited.get(k, 0) >= cnt:
                        continue
                    waited[k] = cnt
                    engobj.wait_ge(d.sem, cnt)
                if o.fn is None:
                    continue
                r = o.fn(engobj)
                if o.signal:
                    if o.is_dma:
                        if not isinstance(r, (list, tuple)):
                            r = [r]
                        assert len(r) == o.ninc, (len(r), o.ninc)
                        for ins in r:
                            ins.then_inc(o.sem, 16)
                    else:
                        if isinstance(r, (list, tuple)):
                            r = r[-1]
                        r.then_inc(o.sem, 1)

        with nc.Block() as block:
            @block.sync
            def _(eng):
                run("sp", eng)

            @block.tensor
            def _(eng):
                run("pe", eng)

            @block.vector
            def _(eng):
                run("dve", eng)

            @block.scalar
            def _(eng):
                run("act", eng)

            @block.gpsimd
            def _(eng):
                run("pool", eng)


class Ring:
    def __init__(self, tiles):
        self.t = tiles
        self.i = 0

    def next(self):
        r = self.t[self.i % len(self.t)]
        self.i += 1
        return r


class K:
    def __init__(self, nc):
        self.nc = nc
        self.P = Prog(nc)
        self.uid = 0

    def sb(self, es, shape, dt, name="t"):
        self.uid += 1
        t = es.enter_context(self.nc.sbuf_tensor(f"{name}{self.uid}", list(shape), dt))
        return t.ap(), Buf(name)

    def ps(self, es, shape, dt=F32, name="p"):
        self.uid += 1
        t = es.enter_context(self.nc.psum_tensor(f"{name}{self.uid}", list(shape), dt))
        return t.ap(), Buf(name)

    def sbring(self, es, shape, dt, n, name="r"):
        return Ring([self.sb(es, shape, dt, name) for _ in range(n)])

    def psring(self, es, shape, dt, n, name="pr"):
        return Ring([self.ps(es, shape, dt, name) for _ in range(n)])

    def dram(self, shape, dt, name):
        if getattr(self, "dbg_scratch", False):
            return self.nc.dram_tensor(name, list(shape), dt, kind="ExternalOutput").ap(), BufMap(name)
        return self.nc.dram_tensor(name, list(shape), dt).ap(), BufMap(name)

    def dma(self, out, in_, rd, wr, eng="sp", **kw):
        return self.P.op(eng, lambda e: e.dma_start(out=out, in_=in_, **kw), reads=rd, writes=wr, dma=True)

    def mm(self, out, lhsT, rhs, rd, wr, start=True, stop=True):
        return self.P.op("pe", lambda e: e.matmul(out, lhsT=lhsT, rhs=rhs, start=start, stop=stop), reads=rd, writes=wr)

    def tr(self, out, in_, rd, wr):
        ident = self.ident
        return self.P.op("pe", lambda e: e.transpose(out=out, in_=in_, identity=ident), reads=rd + [self.b_ident], writes=wr)

    def act(self, out, in_, func, rd, wr, **kw):
        return self.P.op("act", lambda e: e.activation(out=out, in_=in_, func=func, **kw), reads=rd, writes=wr)

    def tt(self, out, in0, in1, op, rd, wr, eng="dve"):
        return self.P.op(eng, lambda e: e.tensor_tensor(out=out, in0=in0, in1=in1, op=op), reads=rd, writes=wr)

    def ts(self, out, in0, s1, s2, op0, op1, rd, wr, eng="dve"):
        if op1 is None:
            return self.P.op(eng, lambda e: e.tensor_scalar(out=out, in0=in0, scalar1=s1, scalar2=None, op0=op0), reads=rd, writes=wr)
        return self.P.op(eng, lambda e: e.tensor_scalar(out=out, in0=in0, scalar1=s1, scalar2=s2, op0=op0, op1=op1), reads=rd, writes=wr)

    def stt(self, out, in0, scalar, in1, op0, op1, rd, wr):
        return self.P.op("dve", lambda e: e.scalar_tensor_tensor(out=out, in0=in0, scalar=scalar, in1=in1, op0=op0, op1=op1), reads=rd, writes=wr)

    def copy(self, out, in_, rd, wr, eng="dve"):
        if eng == "act":
            return self.P.op("act", lambda e: e.copy(out=out, in_=in_), reads=rd, writes=wr)
        return self.P.op(eng, lambda e: e.tensor_copy(out=out, in_=in_), reads=rd, writes=wr)

    def iota(self, ap, pattern, cm, wr):
        return self.P.op("pool", lambda e: e.iota(ap, pattern=pattern, base=0, channel_multiplier=cm), writes=wr)

    def memset(self, ap, val, wr, eng="pool"):
        return self.P.op(eng, lambda e: e.memset(ap, val), writes=wr)

    def asel(self, ap, buf, step, cm, cmp, fill):
        return self.P.op("pool", lambda e: e.affine_select(out=ap, in_=ap, pattern=[[step, ap.shape[-1]]], compare_op=cmp,
                                                            fill=fill, base=0, channel_multiplier=cm), reads=[buf], writes=[buf])

    def load_w(self, dst, b_dst, src, K_, c0=None, c1=None):
        nk = K_ // 128
        s = src if c0 is None else src[:, c0:c1]

        def fn(e):
            return e.dma_start(out=dst, in_=s.rearrange("(k p) f -> p k f", p=128))
        return self.P.op("pool", fn, writes=[b_dst], dma=True)

    def bcast_load(self, dst, b_dst, row_ap):
        return self.dma(dst, row_ap.partition_broadcast(128), [], [b_dst])


def pipeline(items, stages):
    n, S = len(items), len(stages)
    for step in range(n + S - 1):
        for si in range(S):
            i = step - si
            if 0 <= i < n:
                stages[si](items[i])


def build_program(debug_layers=4, debug_raw=False, skip_last_ffn=False, dbg_scratch=False):
    nc = bass.Bass("TRN2", target_bir_lowering=False)
    kb = K(nc)
    kb.dbg_scratch = dbg_scratch
    P = kb.P

    def din(name, shape):
        return nc.dram_tensor(name, list(shape), F32, kind="ExternalInput").ap()

    xin = din("xin", [NT, D])
    cc = din("cc", [2, D])
    mod_w = din("mod_w", [4, D, 6 * D])
    mod_b = din("mod_b", [4, 6 * D])
    norm1_g = din("norm1_g", [4, D])
    norm2_g = din("norm2_g", [4, D])
    ffn_w1 = din("ffn_w1", [4, D, DFF])
    ffn_w3 = din("ffn_w3", [4, D, DFF])
    ffn_w2 = din("ffn_w2", [4, DFF, D])
    ssd_w_in = din("ssd_w_in", [2, D, 5184])
    ssd_conv_w = din("ssd_conv_w", [2, 5, 3072])
    ssd_conv_b = din("ssd_conv_b", [2, 3072])
    ssd_alog = din("ssd_alog", [2, 64])
    ssd_dtb = din("ssd_dtb", [2, 64])
    ssd_d = din("ssd_d", [2, 32])
    ssd_norm_g = din("ssd_norm_g", [2, 2048])
    ssd_w_out = din("ssd_w_out", [2, 2048, D])
    gmlp_w_in = din("gmlp_w_in", [1, D, 4096])
    gmlp_norm_g = din("gmlp_norm_g", [1, 2048])
    gmlp_w_s = din("gmlp_w_s", [1, 8, 128, 128])
    gmlp_b_s = din("gmlp_b_s", [1, 8, 128])
    gmlp_w_out = din("gmlp_w_out", [1, 2048, D])
    ret_w_in = din("ret_w_in", [1, D, 6144])
    ret_decay = din("ret_decay", [1, 8])
    ret_w_out = din("ret_w_out", [1, 2048, D])
    final_g = din("final_g", [D])
    out = nc.dram_tensor("out", [SEQ, D], F32, kind="ExternalOutput").ap()
    b_out = BufMap("out")

    X, b_X = kb.dram([NT, D], F32, "X")
    MODR, b_MODR = kb.dram([2, 6, D], F32, "MODR")
    RAWW = 4 + CTX + 4 + SEQ + 4
    RAWT, b_RAWT = kb.dram([3072, RAWW], BF16, "RAWT")
    COL = [4, 4 + CTX + 4]
    ZS, b_ZS = kb.dram([NT, 2048], BF16, "ZS")
    DT, b_DT = kb.dram([NT, 64], F32, "DT")
    LA, b_LA = kb.dram([NT, 64], F32, "LA")
    XS, b_XS = kb.dram([NT, 2048], BF16, "XS")
    KB_, b_KB = kb.dram([NT, 1024], BF16, "KB")
    BT, b_BT = kb.dram([4, 128, NT], BF16, "BT")
    CT, b_CT = kb.dram([4, 128, NT], BF16, "CT")
    YB, b_YB = kb.dram([NT, 2048], BF16, "YB")
    YF, b_YF = kb.dram([NT, 2048], BF16, "YF")
    QT, b_QT = kb.dram([NCH, 128, 8, 128], BF16, "QT")
    KT, b_KT = kb.dram([NCH, 128, 8, 128], BF16, "KT")
    VV, b_VV = kb.dram([NT, 2048], BF16, "VV")
    GS, b_GS = kb.dram([NT, 2048], BF16, "GS")

    glob = ExitStack()
    identf, b_identf = kb.sb(glob, [128, 128], F32, "identf")
    kb.ident, kb.b_ident = kb.sb(glob, [128, 128], BF16, "ident")
    ones, b_ones = kb.sb(glob, [128, 128], BF16, "ones")
    tmpc, b_tmpc = kb.sb(glob, [128, 128], F32, "tmpc")
    negh, b_negh = kb.sb(glob, [128, 4], F32, "negh")
    msk = {}
    kb.memset(identf, 0.0, [b_identf])
    kb.asel(identf, b_identf, -1, 1, ALU.not_equal, 1.0)
    kb.copy(kb.ident, identf, [b_identf], [kb.b_ident])
    kb.memset(tmpc, 1.0, [b_tmpc])
    kb.copy(ones, tmpc, [b_tmpc], [b_ones])
    kb.memset(negh, -0.5, [b_negh])
    specs = {
        "Uf": (1, -1, ALU.is_ge, 1.0, 0.0), "SLf": (-1, 1, ALU.is_gt, 1.0, 0.0), "Mf": (1, -1, ALU.is_ge, 0.0, NEG),
        "Ub": (-1, 1, ALU.is_ge, 1.0, 0.0), "SLb": (1, -1, ALU.is_gt, 1.0, 0.0), "Mb": (-1, 1, ALU.is_ge, 0.0, NEG),
    }
    for nm, (st, cm, cmp, base, fill) in specs.items():
        f32t, b_f = kb.sb(glob, [128, 128], F32, nm + "f")
        kb.memset(f32t, base, [b_f])
        kb.asel(f32t, b_f, st, cm, cmp, fill)
        if nm[0] == "M":
            t4, b_4 = kb.sb(glob, [128, 4, 128], BF16, nm + "4")
            for q in range(4):
                kb.copy(t4[:, q, :], f32t, [b_f], [b_4])
            msk[nm] = (f32t, b_f, t4, b_4)
        else:
            tb, b_b = kb.sb(glob, [128, 128], BF16, nm + "b")
            kb.copy(tb, f32t, [b_f], [b_b])
            msk[nm] = (f32t, b_f, tb, b_b)

    for c in range(0, NCH, 2):
        kb.dma(X[c * 128:(c + 2) * 128, :], xin[c * 128:(c + 2) * 128, :], [], [b_X[c], b_X[c + 1]])
    P.barrier()

    def norm_setup(es, width=D, tmp=True, nss=3):
        r = {}
        r["ss"] = kb.sbring(es, [128, 1], F32, nss, "ss")
        r["junk"] = kb.sb(es, [128, width], BF16, "junk")
        if tmp:
            r["tmp"] = kb.sbring(es, [128, D], F32, 1, "ntmp")
        return r

    def rstd_of(r, src, b_src, width, nseg=1):
        ss, b_ss = r["ss"].next() if nseg == 1 else r["ss4"].next()
        junk, b_junk = r["junk"]
        for s in range(nseg):
            kb.act(junk[:, 0:width], src[:, s * width:(s + 1) * width], AF.Square, [b_src], [b_junk, b_ss], accum_out=ss[:, s:s + 1])
        kb.ts(ss, ss, 1.0 / width, EPS, ALU.mult, ALU.add, [b_ss], [b_ss])
        kb.tt(ss, ss, negh[:, 0:nseg], ALU.pow, [b_ss, b_negh], [b_ss], eng="pool")
        return ss, b_ss

    def norm_a(r, xc, b_xc, A, b_A, Bc, b_B, a_ring):
        ss, b_ss = rstd_of(r, xc, b_xc, D)
        tmp, b_tmp = r["tmp"].next()
        kb.stt(tmp, xc, ss, A, ALU.mult, ALU.mult, [b_xc, b_ss, b_A], [b_tmp])
        a, b_a = a_ring.next()
        kb.tt(a, tmp, Bc, ALU.add, [b_tmp, b_B], [b_a], eng="pool")
        return a, b_a

    def trans_a(a, b_a, tp_ring, dst, b_dst):
        tp, b_tp = tp_ring.next()
        for k in range(8):
            kb.tr(tp[:, k, :], a[:, k * 128:(k + 1) * 128], [b_a], [b_tp])
        kb.copy(dst, tp, [b_tp], [b_dst], eng="act")

    def norm_mod_T(r, xc, b_xc, A, b_A, Bc, b_B, a_ring, tp_ring, dst, b_dst):
        a, b_a = norm_a(r, xc, b_xc, A, b_A, Bc, b_B, a_ring)
        trans_a(a, b_a, tp_ring, dst, b_dst)

    class ModC:
        def __init__(self, es, idxs):
            self.idxs = idxs
            self.t = {i: kb.sb(es, [128, D], F32, "modc") for i in idxs}
            self.cur = None

        def get(self, kind, i):
            if kind != self.cur:
                self.cur = kind
                for ii in self.idxs:
                    t, b = self.t[ii]
                    kb.dma(t, MODR[kind, ii, :].partition_broadcast(128), [b_MODR[0]], [b])
            return self.t[i]

    def kind_of(c):
        return 1 if c < 2 else 0

    def load_w_groups(es, src, K_, groups, name):
        F_ = src.shape[1]
        nk = K_ // 128
        w, _ = kb.sb(es, [128, nk, F_], BF16, name)
        bufs = []
        for (c0, c1) in groups:
            b = Buf(name)

            def fn(e, c0=c0, c1=c1):
                return e.dma_start(out=w[:, :, c0:c1], in_=src.rearrange("(k p) f -> p k f", p=128)[:, :, c0:c1])
            P.op("pool", fn, writes=[b], dma=True)
            bufs.append(b)
        return w, bufs

    def load_w_kgroups(es, src, K_, kgroups, name):
        F_ = src.shape[1]
        nk = K_ // 128
        w, _ = kb.sb(es, [128, nk, F_], BF16, name)
        bufs = {}
        for (k0, k1) in kgroups:
            b = Buf(name)

            def fn(e, k0=k0, k1=k1):
                return e.dma_start(out=w[:, k0:k1, :], in_=src.rearrange("(k p) f -> p k f", p=128)[:, k0:k1, :])
            P.op("pool", fn, writes=[b], dma=True)
            for k in range(k0, k1):
                bufs[k] = b
        return w, bufs

    def phase_mod(L):
        with ExitStack() as es:
            cT, b_cT = kb.sb(es, [128, 2, 8], F32, "cT")
            kb.dma(cT, cc.rearrange("r (k p) -> p r k", p=128), [], [b_cT], allow_slow_non_contiguous=True)
            kb.act(cT, cT, AF.Silu, [b_cT], [b_cT])
            cl, b_cl = kb.sb(es, [128, 2, 8, 128], BF16, "cl")
            for r_ in range(2):
                kb.copy(cl[:, r_, :, :], cT[:, r_, :].unsqueeze(2).to_broadcast([128, 8, 128]), [b_cT], [b_cl])
            mb, b_mb = kb.sb(es, [128, 6 * D], F32, "mb")
            kb.bcast_load(mb, b_mb, mod_b[L, :])
            ng, b_ng = kb.sb(es, [128, 2, D], F32, "ng")
            kb.bcast_load(ng[:, 0, :], b_ng, norm1_g[L, :])
            kb.bcast_load(ng[:, 1, :], b_ng, norm2_g[L, :])
            wr = kb.sbring(es, [128, 8, 512], BF16, 4, "mw")
            pr = kb.psring(es, [128, 512], F32, 4, "mp")
            rr = kb.sbring(es, [128, 512], F32, 4, "mr")
            st = {}

            def A(gi):
                w, b_w = wr.next()
                kb.load_w(w, b_w, mod_w[L], D, gi * 512, (gi + 1) * 512)
                st[gi] = (w, b_w)

            def B(gi):
                w, b_w = st.pop(gi)
                piece, half = gi // 2, gi % 2
                for kind in range(2):
                    p_, b_p = pr.next()
                    for k in range(8):
                        kb.mm(p_, cl[:, kind, k, :], w[:, k, :], [b_cl, b_w], [b_p], start=(k == 0), stop=(k == 7))
                    r_, b_r = rr.next()
                    kb.tt(r_, p_, mb[:, gi * 512:(gi + 1) * 512], ALU.add, [b_p, b_mb], [b_r])
                    if piece in (1, 4):
                        kb.stt(r_, r_, 1.0, ng[:, 0 if piece == 1 else 1, half * 512:(half + 1) * 512], ALU.add, ALU.mult, [b_r, b_ng], [b_r])
                    idx = {0: 1, 1: 0, 2: 2, 3: 4, 4: 3, 5: 5}[piece]
                    kb.dma(MODR[kind, idx, half * 512:(half + 1) * 512], r_[0:1, :], [b_r], [b_MODR[0]])
            pipeline(list(range(12)), [A, lambda g: None, B])
        P.barrier()

    def outproj_T(rs, yb16, b_y):
        yT, b_yT = rs["yT"].next()
        for j in range(0, 16, 8):
            tp, b_tp = rs["tp"].next()
            for q in range(8):
                kb.tr(tp[:, q, :], yb16[:, (j + q) * 128:(j + q + 1) * 128], [b_y], [b_tp])
            kb.copy(yT[:, j:j + 8, :], tp, [b_tp], [b_yT], eng="act" if j else "dve")
        return yT, b_yT

    def outproj_mm(rs, c, yT, b_yT, wout, wb, G, b_G):
        t, b_t = rs["ot"].next()
        for half in range(2):
            po, b_po = rs["po"].next()
            for f in range(16):
                kb.mm(po, yT[:, f, :], wout[:, f, half * 512:(half + 1) * 512], [b_yT, wb[f]], [b_po], start=(f == 0), stop=(f == 15))
            kb.tt(t[:, half * 512:(half + 1) * 512], po, G[:, half * 512:(half + 1) * 512], ALU.mult, [b_po, b_G], [b_t])
        kb.dma(X[c * 128:(c + 1) * 128, :], t, [b_t], [b_X[c]], eng="pool", accum_op=ALU.add)

    def outproj_residual(rs, c, yb16, b_y, wout, wb, G, b_G):
        yT, b_yT = outproj_T(rs, yb16, b_y)
        outproj_mm(rs, c, yT, b_yT, wout, wb, G, b_G)

    def outproj_setup(es, wsrc):
        rs = {}
        wout, wb = load_w_kgroups(es, wsrc, 2048, [(0, 4), (4, 8), (8, 12), (12, 16)], "wout")
        rs["wout"] = (wout, wb)
        rs["yT"] = kb.sbring(es, [128, 16, 128], BF16, 2, "yT")
        rs["ot"] = kb.sbring(es, [128, D], F32, 2, "ot")
        return rs

    def phase_ffn(L):
        with ExitStack() as es:
            groups = [(i * 512, (i + 1) * 512) for i in range(5)] + [(2560, 2816)]
            w1, w1b = load_w_groups(es, ffn_w1[L], D, groups[:1], "w1")
            w3, w3b = load_w_groups(es, ffn_w3[L], D, groups[:1], "w3")
            for (c0, c1) in groups[1:]:
                for (w_, wb_, src) in ((w1, w1b, ffn_w1[L]), (w3, w3b, ffn_w3[L])):
                    b = Buf("wg")

                    def fn(e, c0=c0, c1=c1, w_=w_, src=src):
                        return e.dma_start(out=w_[:, :, c0:c1], in_=src.rearrange("(k p) f -> p k f", p=128)[:, :, c0:c1])
                    P.op("pool", fn, writes=[b], dma=True)
                    wb_.append(b)
            w2, w2b = load_w_kgroups(es, ffn_w2[L], DFF, [(0, 6), (6, 12), (12, 17), (17, 22)], "w2")
            mcA = ModC(es, [3, 4])
            mcB = ModC(es, [5])
            nr = norm_setup(es)
            xr = kb.sbring(es, [128, D], F32, 2, "fx")
            ar = kb.sbring(es, [128, D], BF16, 2, "fa")
            aTr = kb.sbring(es, [128, 8, 128], BF16, 2, "faT")
            tpr = kb.psring(es, [128, 8, 128], BF16, 2, "ftp")
            hr = kb.psring(es, [128, 512], F32, 4, "fh")
            por = kb.psring(es, [128, 512], F32, 2, "fpo")
            sr = kb.sbring(es, [128, 512], F32, 2, "fs")
            gr = kb.sbring(es, [128, DFF], BF16, 2, "fg")
            gTr = kb.sbring(es, [128, 22, 128], BF16, 2, "fgT")
            otr = kb.sbring(es, [128, D], F32, 2, "fot")
            st = {}

            def L_(c):
                xc, b_xc = xr.next()
                kb.dma(xc, X[c * 128:(c + 1) * 128, :], [b_X[c]], [b_xc])
                st[("x", c)] = (xc, b_xc)

            def N_(c):
                kd = kind_of(c)
                xc, b_xc = st.pop(("x", c))
                st[("a", c)] = norm_a(nr, xc, b_xc, *mcA.get(kd, 3), *mcA.get(kd, 4), ar)

            def A(c):
                a, b_a = st.pop(("a", c))
                aT, b_aT = aTr.next()
                trans_a(a, b_a, tpr, aT, b_aT)
                st[c] = (None, None, aT, b_aT)

            def B1(c):
                xc, b_xc, aT, b_aT = st.pop(c)
                g, b_g = gr.next()
                for gi, (f0, f1) in enumerate(groups):
                    fw = f1 - f0
                    h1, b_h1 = hr.next()
                    h3, b_h3 = hr.next()
                    for k in range(8):
                        kb.mm(h1[:, :fw], aT[:, k, :], w1[:, k, f0:f1], [b_aT, w1b[gi]], [b_h1], start=(k == 0), stop=(k == 7))
                    for k in range(8):
                        kb.mm(h3[:, :fw], aT[:, k, :], w3[:, k, f0:f1], [b_aT, w3b[gi]], [b_h3], start=(k == 0), stop=(k == 7))
                    s, b_s = sr.next()
                    kb.act(s[:, :fw], h1[:, :fw], AF.Silu, [b_h1], [b_s])
                    kb.tt(g[:, f0:f1], s[:, :fw], h3[:, :fw], ALU.mult, [b_s, b_h3], [b_g])
                st[("g", c)] = (g, b_g)

            def B2(c):
                g, b_g = st.pop(("g", c))
                gT, b_gT = gTr.next()
                for j in range(0, 22, 8):
                    nb = min(8, 22 - j)
                    tp, b_tp = tpr.next()
                    for q in range(nb):
                        kb.tr(tp[:, q, :], g[:, (j + q) * 128:(j + q + 1) * 128], [b_g], [b_tp])
                    kb.copy(gT[:, j:j + nb, :], tp[:, :nb, :], [b_tp], [b_gT], eng="act" if j == 8 else "dve")
                st[("gT", c)] = (gT, b_gT)

            def B3(c):
                kd = kind_of(c)
                gT, b_gT = st.pop(("gT", c))
                G, b_G = mcB.get(kd, 5)
                t, b_t = otr.next()
                for half in range(2):
                    po, b_po = por.next()
                    for f in range(22):
                        kb.mm(po, gT[:, f, :], w2[:, f, half * 512:(half + 1) * 512], [b_gT, w2b[f]], [b_po], start=(f == 0), stop=(f == 21))
                    kb.tt(t[:, half * 512:(half + 1) * 512], po, G[:, half * 512:(half + 1) * 512], ALU.mult, [b_po, b_G], [b_t])
                kb.dma(X[c * 128:(c + 1) * 128, :], t, [b_t], [b_X[c]], eng="pool", accum_op=ALU.add)
            pipeline(list(range(2 if L == 3 else 0, NCH)), [L_, N_, A, B1, B2, B3])
        P.barrier()

    def phase_gmlp(L):
        with ExitStack() as es:
            win, winb = load_w_groups(es, gmlp_w_in[0], D, [(i * 512, (i + 1) * 512) for i in range(8)], "gwin")
            rs = outproj_setup(es, gmlp_w_out[0])
            wout, wb = rs["wout"]
            mcA = ModC(es, [0, 1])
            mcB = ModC(es, [2])
            nr = norm_setup(es, 2048)
            ngb, b_ngb = kb.sb(es, [128, 2048], F32, "gng")
            kb.bcast_load(ngb, b_ngb, gmlp_norm_g[0, :])
            bs, b_bs = kb.sb(es, [128, 8], F32, "gbs")
            kb.dma(bs, gmlp_b_s[0].rearrange("g i -> i g"), [], [b_bs], allow_slow_non_contiguous=True)
            wsT, b_wsT = kb.sb(es, [128, 8, 128], BF16, "wsT")
            tpr = kb.psring(es, [128, 8, 128], BF16, 2, "gtp")
            rs["tp"] = tpr
            hr = kb.psring(es, [128, 512], F32, 2, "gh")
            rs["po"] = hr
            pvr = kb.psring(es, [128, 512], F32, 4, "gpv")
            with ExitStack() as es2:
                wsf, b_wsf = kb.sb(es2, [128, 8, 128], F32, "wsf")
                kb.dma(wsf, gmlp_w_s[0].rearrange("g i j -> i g j"), [], [b_wsf])
                wsb, b_wsb = kb.sb(es2, [128, 8, 128], BF16, "wsb")
                kb.copy(wsb, wsf, [b_wsf], [b_wsb])
                tp, b_tp = tpr.next()
                for g_ in range(8):
                    kb.tr(tp[:, g_, :], wsb[:, g_, :], [b_wsb], [b_tp])
                kb.copy(wsT, tp, [b_tp], [b_wsT])
                P.barrier()
            xr = kb.sbring(es, [128, D], F32, 3, "gx")
            ar = kb.sbring(es, [128, D], BF16, 2, "ga")
            aTr = kb.sbring(es, [128, 8, 128], BF16, 2, "gaT")
            ur = kb.sbring(es, [128, 2048], BF16, 2, "gu")
            vr = kb.sbring(es, [128, 2048], F32, 1, "gv")
            vnr = kb.sbring(es, [128, 2048], BF16, 2, "gvn")
            gtr = kb.sbring(es, [128, 2048], BF16, 2, "ggt")
            st = {}

            def L_(c):
                xc, b_xc = xr.next()
                kb.dma(xc, X[c * 128:(c + 1) * 128, :], [b_X[c]], [b_xc])
                st[("x", c)] = (xc, b_xc)

            def N_(c):
                kd = kind_of(c)
                xc, b_xc = st.pop(("x", c))
                st[("a", c)] = norm_a(nr, xc, b_xc, *mcA.get(kd, 0), *mcA.get(kd, 1), ar)

            def A(c):
                a, b_a = st.pop(("a", c))
                aT, b_aT = aTr.next()
                trans_a(a, b_a, tpr, aT, b_aT)
                st[c] = (None, None, aT, b_aT)

            def B1(c):
                xc, b_xc, aT, b_aT = st.pop(c)
                u, b_u = ur.next()
                v, b_v = vr.next()
                for gi in [4, 5, 6, 7, 0, 1, 2, 3]:
                    h, b_h = hr.next()
                    for k in range(8):
                        kb.mm(h, aT[:, k, :], win[:, k, gi * 512:(gi + 1) * 512], [b_aT, winb[gi]], [b_h], start=(k == 0), stop=(k == 7))
                    if gi < 4:
                        kb.act(u[:, gi * 512:(gi + 1) * 512], h, AF.Gelu_apprx_tanh, [b_h], [b_u])
                    else:
                        kb.act(v[:, (gi - 4) * 512:(gi - 3) * 512], h, AF.Gelu_apprx_tanh, [b_h], [b_v])
                ss, b_ss = rstd_of(nr, v, b_v, 2048)
                vn, b_vn = vnr.next()
                kb.stt(vn, v, ss, ngb, ALU.mult, ALU.mult, [b_v, b_ss, b_ngb], [b_vn])
                st[("u", c)] = (u, b_u, vn, b_vn)

            def B2(c):
                u, b_u, vn, b_vn = st.pop(("u", c))
                gt, b_gt = gtr.next()
                for q in range(4):
                    pv, b_pv = pvr.next()
                    for hh in range(2):
                        g_ = 2 * q + hh
                        kb.mm(pv[:, hh * 256:(hh + 1) * 256], wsT[:, g_, :], vn[:, g_ * 256:(g_ + 1) * 256], [b_wsT, b_vn], [b_pv])
                    for hh in range(2):
                        g_ = 2 * q + hh
                        kb.stt(gt[:, g_ * 256:(g_ + 1) * 256], pv[:, hh * 256:(hh + 1) * 256], bs[:, g_:g_ + 1], u[:, g_ * 256:(g_ + 1) * 256],
                               ALU.add, ALU.mult, [b_pv, b_bs, b_u], [b_gt])
                st[("gt", c)] = (gt, b_gt)

            def B3(c):
                gt, b_gt = st.pop(("gt", c))
                st[("yT", c)] = outproj_T(rs, gt, b_gt)

            def B4(c):
                kd = kind_of(c)
                yT, b_yT = st.pop(("yT", c))
                outproj_mm(rs, c, yT, b_yT, wout, wb, *mcB.get(kd, 2))
            pipeline(list(range(NCH)), [L_, N_, A, B1, B2, B3, B4])
        P.barrier()

    def phase_finish(kindname, j, wsrc, skip_ctx=False):
        ssd = kindname == "ssd"
        with ExitStack() as es:
            rs = outproj_setup(es, wsrc)
            wout, wb = rs["wout"]
            rs["tp"] = kb.psring(es, [128, 8, 128], BF16, 2, "ntp")
            rs["po"] = kb.psring(es, [128, 512], F32, 2, "npo")
            psr = kb.psring(es, [128, 512], F32, 4, "nps")
            mc = ModC(es, [2])
            if ssd:
                nr = norm_setup(es, 2048, tmp=False)
                ngb, b_ngb = kb.sb(es, [128, 2048], F32, "sng")
                kb.bcast_load(ngb, b_ngb, ssd_norm_g[j, :])
                Db, b_Db = kb.sb(es, [128, 32], F32, "Db")
                kb.bcast_load(Db, b_Db, ssd_d[j, :])
                xsr = kb.sbring(es, [128, 32, 64], BF16, 3, "nxs")
                xdr = kb.sbring(es, [128, 2048], BF16, 2, "nxd")
            else:
                nr = {"ss4": kb.sbring(es, [128, 4], F32, 3, "ss4"), "junk": kb.sb(es, [128, 512], BF16, "wjunk")}
            yfr = kb.sbring(es, [128, 2048], BF16, 3, "nyf")
            ybr = kb.sbring(es, [128, 2048], BF16, 3, "nyb")
            gzr = kb.sbring(es, [128, 2048], BF16, 3, "ngz")
            y2r = kb.sbring(es, [128, 2048], F32, 3, "ny2")
            y3r = kb.sbring(es, [128, 2048], BF16, 4, "ny3")
            rs["yT"] = kb.sbring(es, [128, 16, 128], BF16, 3, "yT3")
            if not ssd:
                nr["ss4"] = kb.sbring(es, [128, 4], F32, 4, "ss4b")
            st = {}

            def A(c):
                rows = slice(c * 128, (c + 1) * 128)
                yf, b_yf = yfr.next()
                kb.dma(yf, YF[rows, :], [b_YF[c]], [b_yf])
                yb_, b_yb = ybr.next()
                kb.dma(yb_, YB[rows, :], [b_YB[c]], [b_yb])
                gz, b_gz = gzr.next()
                kb.dma(gz, (ZS if ssd else GS)[rows, :], [(b_ZS if ssd else b_GS)[c]], [b_gz])
                xs = b_xs = None
                if ssd:
                    xs, b_xs = xsr.next()
                    kb.dma(xs, XS[rows, :].rearrange("p (h d) -> p h d", d=64), [b_XS[c]], [b_xs])
                st[c] = (yf, b_yf, yb_, b_yb, gz, b_gz, xs, b_xs)

            def Ba(c):
                yf, b_yf, yb_, b_yb, gz, b_gz, xs, b_xs = st.pop(c)
                if ssd:
                    y2, b_y2 = y2r.next()
                    xd, b_xd = xdr.next()
                    kb.tt(xd.rearrange("p (h d) -> p h d", d=64), xs, Db.unsqueeze(2).to_broadcast([128, 32, 64]), ALU.mult, [b_xs, b_Db], [b_xd])
                    for q in range(4):
                        cs_ = slice(q * 512, (q + 1) * 512)
                        ps, b_ps = psr.next()
                        kb.mm(ps, kb.ident, yf[:, cs_], [kb.b_ident, b_yf], [b_ps], start=True, stop=False)
                        kb.mm(ps, kb.ident, yb_[:, cs_], [kb.b_ident, b_yb], [b_ps], start=False, stop=False)
                        kb.mm(ps, kb.ident, xd[:, cs_], [kb.b_ident, b_xd], [b_ps], start=False, stop=True)
                        kb.tt(y2[:, cs_], ps, gz[:, cs_], ALU.mult, [b_ps, b_gz], [b_y2])
                    st[("y2", c)] = (y2, b_y2, None, None, None, None)
                else:
                    ss, b_ss = nr["ss4"].next()
                    junk, b_junk = nr["junk"]
                    y3, b_y3 = y3r.next()
                    pss = []
                    for q in range(4):
                        cs_ = slice(q * 512, (q + 1) * 512)
                        ps, b_ps = psr.next()
                        kb.mm(ps, kb.ident, yf[:, cs_], [kb.b_ident, b_yf], [b_ps], start=True, stop=False)
                        kb.mm(ps, kb.ident, yb_[:, cs_], [kb.b_ident, b_yb], [b_ps], start=False, stop=True)
                        kb.act(junk, ps, AF.Square, [b_ps], [b_junk, b_ss], accum_out=ss[:, q:q + 1])
                        pss.append((ps, b_ps))
                    kb.ts(ss, ss, 1.0 / 512, EPS, ALU.mult, ALU.add, [b_ss], [b_ss])
                    kb.tt(ss, ss, negh[:, 0:4], ALU.pow, [b_ss, b_negh], [b_ss], eng="pool")
                    for q, (ps, b_ps) in enumerate(pss):
                        cs_ = slice(q * 512, (q + 1) * 512)
                        kb.stt(y3[:, cs_], ps, ss[:, q:q + 1], gz[:, cs_], ALU.mult, ALU.mult, [b_ps, b_ss, b_gz], [b_y3])
                    st[("y2", c)] = (y3, b_y3, None, None, None, None)

            def Bb(c):
                y2, b_y2, ss, b_ss, gz, b_gz = st.pop(("y2", c))
                if ssd:
                    y3, b_y3 = y3r.next()
                    ss, b_ss = rstd_of(nr, y2, b_y2, 2048)
                    kb.stt(y3, y2, ss, ngb, ALU.mult, ALU.mult, [b_y2, b_ss, b_ngb], [b_y3])
                else:
                    y3, b_y3 = y2, b_y2
                st[("y", c)] = (y3, b_y3)

            def B2a(c):
                y3, b_y3 = st.pop(("y", c))
                st[("yT", c)] = outproj_T(rs, y3, b_y3)

            def B2b(c):
                kd = kind_of(c)
                yT, b_yT = st.pop(("yT", c))
                outproj_mm(rs, c, yT, b_yT, wout, wb, *mc.get(kd, 2))
            pipeline(list(range(2 if skip_ctx else 0, NCH)), [A, Ba, Bb, B2a, B2b])
        P.barrier()

    def phase_ssd(L, j):
        with ExitStack() as es:
            wg = [(i * 512, (i + 1) * 512) for i in range(10)] + [(5120, 5184)]
            order = [0, 1, 2, 3, 10, 4, 5, 6, 7, 8, 9]
            win, winb_l = load_w_groups(es, ssd_w_in[j], D, [wg[i] for i in order], "swin")
            winb = {order[i]: winb_l[i] for i in range(len(order))}
            mc = ModC(es, [0, 1])
            nr = norm_setup(es)
            dtb, b_dtb = kb.sb(es, [128, 64], F32, "dtb")
            kb.bcast_load(dtb, b_dtb, ssd_dtb[j, :])
            Ab, b_Ab = kb.sb(es, [128, 64], F32, "Ab")
            kb.bcast_load(Ab, b_Ab, ssd_alog[j, :])
            kb.act(Ab, Ab, AF.Exp, [b_Ab], [b_Ab])
            kb.ts(Ab, Ab, -1.0, None, ALU.mult, None, [b_Ab], [b_Ab])
            zt, b_zt = kb.sb(es, [128, 8], BF16, "zt")
            kb.memset(zt, 0.0, [b_zt])
            for fc in range(24):
                rowsl = slice(fc * 128, (fc + 1) * 128)
                kb.dma(RAWT[rowsl, 0:4], zt[:, 0:4], [b_zt], [b_RAWT[fc]])
                kb.dma(RAWT[rowsl, 4 + CTX:4 + CTX + 8], zt, [b_zt], [b_RAWT[fc]])
                kb.dma(RAWT[rowsl, RAWW - 4:RAWW], zt[:, 0:4], [b_zt], [b_RAWT[fc]])
            xr = kb.sbring(es, [128, D], F32, 3, "sx")
            ar = kb.sbring(es, [128, D], BF16, 8, "sa")
            aTt = kb.sbring(es, [128, 8, 512], BF16, 2, "saT")
            tpr = kb.psring(es, [128, 8, 128], BF16, 2, "stp")
            hr = kb.psring(es, [128, 512], F32, 3, "sh")
            pdt = kb.psring(es, [128, 4, 64], F32, 1, "spd")
            zr = kb.sbring(es, [128, 2048], BF16, 2, "sz")
            rawr = kb.sbring(es, [128, 512], BF16, 4, "sraw")
            dr = kb.sbring(es, [128, 5, 4, 64], F32, 2, "sd")
            tiles = [(0, 2)] + [(2 + 4 * t, 4) for t in range(8)]
            st = {}
            cw, b_cw = kb.sb(es, [128, 24, 5], F32, "cw")
            for k in range(5):
                kb.dma(cw[:, :, k], ssd_conv_w[j, k, :].rearrange("(f p) -> p f", p=128), [], [b_cw], allow_slow_non_contiguous=True)
            cb, b_cb = kb.sb(es, [128, 24], F32, "cb")
            kb.dma(cb, ssd_conv_b[j, :].rearrange("(f p) -> p f", p=128), [], [b_cb], allow_slow_non_contiguous=True)
            NW = 3
            rwr = kb.sbring(es, [128, 516], BF16, 3 * NW, "crw")
            accr = kb.sbring(es, [128, 512], F32, 2 * NW, "cacc")
            actr = kb.sbring(es, [128, 512], BF16, 3 * NW, "cact")
            ctpr = kb.psring(es, [128, 4, 128], BF16, 2, "ctp")
            str_ = kb.sbring(es, [128, 4, 128], BF16, 2 * NW, "cst")

            def N_(ti):
                c0, ncn = tiles[ti]
                l = []
                for q in range(ncn):
                    c = c0 + q
                    kd = kind_of(c)
                    xc, b_xc = xr.next()
                    kb.dma(xc, X[c * 128:(c + 1) * 128, :], [b_X[c]], [b_xc])
                    l.append(norm_a(nr, xc, b_xc, *mc.get(kd, 0), *mc.get(kd, 1), ar))
                st[("a", ti)] = l

            def A(ti):
                c0, ncn = tiles[ti]
                aT, b_aT = aTt.next()
                for q, (a, b_a) in enumerate(st.pop(("a", ti))):
                    trans_a(a, b_a, tpr, aT[:, :, q * 128:(q + 1) * 128], b_aT)
                st[ti] = (aT, b_aT)

            def B(ti):
                c0, ncn = tiles[ti]
                aT, b_aT = st.pop(ti)
                TT = ncn * 128
                seg = 0 if ti == 0 else 1
                col0 = COL[seg] + (0 if ti == 0 else (ti - 1) * 512)
                for q in range(ncn):
                    c = c0 + q
                    z, b_z = zr.next()
                    for gi in range(4):
                        h, b_h = hr.next()
                        for k in range(8):
                            kb.mm(h, aT[:, k, q * 128:(q + 1) * 128], win[:, k, gi * 512:(gi + 1) * 512], [b_aT, winb[gi]], [b_h], start=(k == 0), stop=(k == 7))
                        kb.act(z[:, gi * 512:(gi + 1) * 512], h, AF.Silu, [b_h], [b_z])
                    kb.dma(ZS[c * 128:(c + 1) * 128, :], z, [b_z], [b_ZS[c]])
                    yield
                pd, b_pd = pdt.next()
                for q in range(ncn):
                    for k in range(8):
                        kb.mm(pd[:, q, :], aT[:, k, q * 128:(q + 1) * 128], win[:, k, 5120:5184], [b_aT, winb[10]], [b_pd], start=(k == 0), stop=(k == 7))
                d_, b_d = dr.next()
                n_ = ncn
                kb.tt(d_[:, 0, :n_, :], pd[:, :n_, :], dtb.unsqueeze(1).to_broadcast([128, n_, 64]), ALU.add, [b_pd, b_dtb], [b_d])
                kb.act(d_[:, 1, :n_, :], d_[:, 0, :n_, :], AF.Abs, [b_d], [b_d])
                kb.act(d_[:, 1, :n_, :], d_[:, 1, :n_, :], AF.Exp, [b_d], [b_d], scale=-1.0)
                kb.act(d_[:, 1, :n_, :], d_[:, 1, :n_, :], AF.Ln, [b_d], [b_d], bias=1.0)
                kb.stt(d_[:, 2, :n_, :], d_[:, 0, :n_, :], 0.0, d_[:, 1, :n_, :], ALU.max, ALU.add, [b_d], [b_d])
                kb.tt(d_[:, 3, :n_, :], d_[:, 2, :n_, :], Ab.unsqueeze(1).to_broadcast([128, n_, 64]), ALU.mult, [b_d, b_Ab], [b_d])
                for q in range(ncn):
                    c = c0 + q
                    kb.dma(DT[c * 128:(c + 1) * 128, :], d_[:, 2, q, :], [b_d], [b_DT[c]])
                    kb.dma(LA[c * 128:(c + 1) * 128, :], d_[:, 3, q, :], [b_d], [b_LA[c]])
                yield
                for fc in range(24):
                    h, b_h = hr.next()
                    gi = 4 + fc // 4
                    for k in range(8):
                        kb.mm(h[:, :TT], win[:, k, 2048 + fc * 128:2048 + (fc + 1) * 128], aT[:, k, :TT], [b_aT, winb[gi]], [b_h], start=(k == 0), stop=(k == 7))
                    rw, b_rw = rawr.next()
                    kb.copy(rw[:, :TT], h[:, :TT], [b_h], [b_rw], eng="act")
                    kb.dma(RAWT[fc * 128:(fc + 1) * 128, col0:col0 + TT], rw[:, :TT], [b_rw], [b_RAWT[fc]])
                    yield
            def CV(ti):
                c0, ncn = tiles[ti]
                TT = ncn * 128
                seg = 0 if ti == 0 else 1
                off = 0 if ti == 0 else (ti - 1) * 512
                col0 = COL[seg] + off
                t0 = (0 if seg == 0 else CTX) + off
                nb = TT // 128
                items = [list(range(f, f + NW)) for f in range(0, 24, NW)]
                st2 = {}

                def A2(fcs):
                    l = []
                    for fc in fcs:
                        rw, b_rw = rwr.next()
                        kb.dma(rw[:, :TT + 4], RAWT[fc * 128:(fc + 1) * 128, col0 - 2:col0 + TT + 2], [b_RAWT[fc]], [b_rw])
                        l.append((rw, b_rw))
                    st2[("A", fcs[0])] = l

                def B2(fcs):
                    l = st2.pop(("A", fcs[0]))
                    accs = [accr.next() for _ in fcs]
                    for k in range(5):
                        for fc, (rw, b_rw), (acc, b_acc) in zip(fcs, l, accs):
                            if k == 0:
                                kb.ts(acc[:, :TT], rw[:, 0:TT], cw[:, fc, 0:1], cb[:, fc:fc + 1], ALU.mult, ALU.add, [b_rw, b_cw, b_cb], [b_acc])
                            else:
                                kb.stt(acc[:, :TT], rw[:, k:k + TT], cw[:, fc, k:k + 1], acc[:, :TT], ALU.mult, ALU.add, [b_rw, b_cw, b_acc], [b_acc])
                    l2 = []
                    for fc, (acc, b_acc) in zip(fcs, accs):
                        a_, b_a = actr.next()
                        kb.act(a_[:, :TT], acc[:, :TT], AF.Silu, [b_acc], [b_a])
                        l2.append((a_, b_a))
                    st2[("B", fcs[0])] = l2

                def C2(fcs):
                    l2 = st2.pop(("B", fcs[0]))
                    for fc, (a_, b_a) in zip(fcs, l2):
                        if fc < 20:
                            tp, b_tp = ctpr.next()
                            for q in range(nb):
                                kb.tr(tp[:, q, :], a_[:, q * 128:(q + 1) * 128], [b_a], [b_tp])
                            s_, b_st = str_.next()
                            kb.copy(s_[:, :nb, :], tp[:, :nb, :], [b_tp], [b_st], eng="act")
                            if fc < 16:
                                dst = XS[t0:t0 + TT, fc * 128:(fc + 1) * 128]
                                bd = [b_XS[t0 // 128 + q] for q in range(nb)]
                            else:
                                dst = KB_[t0:t0 + TT, (fc - 16) * 128:(fc - 15) * 128]
                                bd = [b_KB[t0 // 128 + q] for q in range(nb)]
                            kb.dma(dst.rearrange("(b p) f -> p b f", p=128), s_[:, :nb, :], [b_st], bd)
                        if 16 <= fc < 20:
                            kb.dma(BT[fc - 16, :, t0:t0 + TT], a_[:, :TT], [b_a], [b_BT[t0 // 128 + q] for q in range(nb)])
                        if fc >= 20:
                            kb.dma(CT[fc - 20, :, t0:t0 + TT], a_[:, :TT], [b_a], [b_CT[t0 // 128 + q] for q in range(nb)])
                stages2 = [A2, B2, lambda f: None, C2]
                n2, S2 = len(items), len(stages2)
                for step in range(n2 + S2 - 1):
                    for si in range(S2):
                        i = step - si
                        if 0 <= i < n2:
                            stages2[si](items[i])
                    yield
            nt = len(tiles)
            for s_ in range(nt + 4):
                if s_ < nt:
                    N_(s_)
                if 0 <= s_ - 1 < nt:
                    A(s_ - 1)
                gb = B(s_ - 2) if 0 <= s_ - 2 < nt else None
                gc = CV(s_ - 4) if 0 <= s_ - 4 < nt else None
                while gb is not None or gc is not None:
                    if gc is not None:
                        try:
                            next(gc)
                        except StopIteration:
                            gc = None
                    for _ in range(3):
                        if gb is None:
                            break
                        try:
                            next(gb)
                        except StopIteration:
                            gb = None
        P.barrier()
        with ExitStack() as es:
            pseg = kb.psring(es, [128, 4, 128], F32, 3, "ppseg")
            pyr = kb.psring(es, [128, 8, 64], F32, 1, "ppy")
            pir = kb.psring(es, [128, 8, 64], F32, 1, "ppi")
            pstr = kb.psring(es, [128, 512], F32, 1, "ppst")

            def make_sweep(sweep):
                fin = sweep == "f"
                U = msk["U" + sweep]
                SL = msk["SL" + sweep]
                M4 = msk["M" + sweep]
                hs = 0 if sweep == "f" else 32
                YD, b_YD = (YF, b_YF) if fin else (YB, b_YB)
                S, b_S = kb.sb(es, [128, 32, 64], F32, "S")
                Sb, b_Sb = kb.sb(es, [128, 2048], BF16, "Sb")
                kb.memset(S, 0.0, [b_S])
                kb.memset(Sb, 0.0, [b_Sb])
                bankA, _ = kb.ps(es, [128, 512], F32, "bankA")
                psc = bankA[:, 0:256].rearrange("p (g t) -> p g t", t=128)
                b_psc = Buf("psc")
                pcv = bankA[:, 256:352]
                b_pcv = Buf("pc")
                xsr = kb.sbring(es, [128, 32, 64], BF16, 3, "qxs")
                kr = kb.sbring(es, [128, 512], BF16, 5, "qk")
                btr = kb.sbring(es, [128, 4, 128], BF16, 4, "qbt")
                ctr = kb.sbring(es, [128, 4, 128], BF16, 5, "qct")
                dtr = kb.sbring(es, [128, 2, 64], F32, 4, "qdt")
                la16r = kb.sbring(es, [128, 32], BF16, 2, "qla")
                r1ar = kb.sbring(es, [128, 16, 128], BF16, 2, "qr1a")
                r1br = kb.sbring(es, [128, 16, 128], BF16, 2, "qr1b")
                exr = kb.sbring(es, [128, 96], F32, 4, "qex")
                vdr = kb.sbring(es, [128, 32, 64], BF16, 2, "qvd")
                vtr = kb.sbring(es, [128, 32, 64], BF16, 2, "qvt")
                scr = kb.sbring(es, [128, 4, 128], BF16, 2, "qsc")
                Er = kb.sbring(es, [128, 4, 128], BF16, 3, "qE")
                atr = kb.sbring(es, [128, 4, 128], BF16, 3, "qat")
                t1r = kb.sbring(es, [128, 8, 64], F32, 2, "qt1")
                ydr = kb.sbring(es, [128, 2048], BF16, 2, "qyd")
                order = list(range(NCH)) if fin else [1, 0] + list(range(NCH - 1, 1, -1))
                st = {}

                def A(c):
                    rows = slice(c * 128, (c + 1) * 128)
                    xs, b_xs = xsr.next()
                    kb.dma(xs, XS[rows, :].rearrange("p (h d) -> p h d", d=64), [b_XS[c]], [b_xs])
                    kk, b_kk = kr.next()
                    kb.dma(kk, KB_[rows, 0:512], [b_KB[c]], [b_kk])
                    bt, b_bt = btr.next()
                    kb.dma(bt, BT[:, :, rows].rearrange("g n t -> n g t"), [b_BT[c]], [b_bt])
                    ct, b_ct = ctr.next()
                    kb.dma(ct, CT[:, :, rows].rearrange("g n t -> n g t"), [b_CT[c]], [b_ct])
                    dl, b_dl = dtr.next()
                    kb.dma(dl[:, 0, :], DT[rows, :], [b_DT[c]], [b_dl])
                    kb.dma(dl[:, 1, :], LA[rows, :], [b_LA[c]], [b_dl])
                    st[c] = dict(xs=xs, b_xs=b_xs, kk=kk, b_kk=b_kk, bt=bt, b_bt=b_bt, ct=ct, b_ct=b_ct, dl=dl, b_dl=b_dl)
                    yield

                def B(c):
                    s_ = st[c]
                    dl, b_dl, xs, b_xs, bt, b_bt, ct, b_ct = s_["dl"], s_["b_dl"], s_["xs"], s_["b_xs"], s_["bt"], s_["b_bt"], s_["ct"], s_["b_ct"]
                    dtv = dl[:, 0, hs:hs + 32]
                    la = dl[:, 1, hs:hs + 32]
                    la16, b_la16 = la16r.next()
                    kb.copy(la16, la, [b_dl], [b_la16], eng="act")
                    kb.mm(pcv[:, 0:32], U[2], la16, [U[3], b_la16], [b_pcv])
                    kb.mm(pcv[:, 32:64], SL[2], la16, [SL[3], b_la16], [b_pcv])
                    kb.mm(pcv[:, 64:96], ones, la16, [b_ones, b_la16], [b_pcv])
                    ex, b_ex = exr.next()
                    kb.act(ex, pcv, AF.Exp, [b_pcv], [b_ex])
                    s_.update(ex=ex, b_ex=b_ex)
                    yield

                def B2_(c):
                    s_ = st[c]
                    dl, b_dl, xs, b_xs, bt, b_bt, ct, b_ct = s_["dl"], s_["b_dl"], s_["xs"], s_["b_xs"], s_["bt"], s_["b_bt"], s_["ct"], s_["b_ct"]
                    ex, b_ex = s_["ex"], s_["b_ex"]
                    dtv = dl[:, 0, hs:hs + 32]
                    la = dl[:, 1, hs:hs + 32]
                    r1a, b_r1a = r1ar.next()
                    r1b, b_r1b = r1br.next()
                    kb.tt(r1a, U[0].unsqueeze(1).to_broadcast([128, 16, 128]), la[:, 0:16].unsqueeze(2).to_broadcast([128, 16, 128]), ALU.mult,
                          [U[1], b_dl], [b_r1a], eng="pool")
                    kb.tt(r1b, U[0].unsqueeze(1).to_broadcast([128, 16, 128]), la[:, 16:32].unsqueeze(2).to_broadcast([128, 16, 128]), ALU.mult,
                          [U[1], b_dl], [b_r1b], eng="dve")
                    s_.update(r1=(r1a, r1b), b_r1=(b_r1a, b_r1b))
                    vd, b_vd = vdr.next()
                    kb.tt(vd, xs, dtv.unsqueeze(2).to_broadcast([128, 32, 64]), ALU.mult, [b_xs, b_dl], [b_vd], eng="pool")
                    vt, b_vt = vtr.next()
                    kb.tt(vt, vd, ex[:, 32:64].unsqueeze(2).to_broadcast([128, 32, 64]), ALU.mult, [b_vd, b_ex], [b_vt], eng="pool")
                    sc, b_sc = scr.next()
                    for half in range(2):
                        for gg in range(2):
                            g_ = 2 * half + gg
                            kb.mm(psc[:, gg, :], bt[:, g_, :], ct[:, g_, :], [b_bt, b_ct], [b_psc])
                        kb.copy(sc[:, 2 * half:2 * half + 2, :], psc, [b_psc], [b_sc], eng="act")
                    s_.update(vd=vd, b_vd=b_vd, vt=vt, b_vt=b_vt, sc=sc, b_sc=b_sc)
                    yield

                def C(c):
                    s_ = st.pop(c)
                    rows = slice(c * 128, (c + 1) * 128)
                    r1, b_r1, ex, b_ex, vd, b_vd, vt, b_vt, sc, b_sc = (s_[k] for k in ("r1", "b_r1", "ex", "b_ex", "vd", "b_vd", "vt", "b_vt", "sc", "b_sc"))
                    ct, b_ct, kk, b_kk = s_["ct"], s_["b_ct"], s_["kk"], s_["b_kk"]
                    ecum, etot = ex[:, 0:32], ex[:, 64:96]
                    yd, b_yd = ydr.next()
                    segs = {}

                    def seg(q):
                        pg, b_pg = pseg.next()
                        kb.mm(pg, SL[2], r1[q // 4][:, 4 * (q % 4):4 * (q % 4) + 4, :], [SL[3], b_r1[q // 4]], [b_pg], start=True, stop=False)
                        kb.mm(pg, kb.ident, M4[2], [kb.b_ident, M4[3]], [b_pg], start=False, stop=True)
                        E, b_E = Er.next()
                        kb.act(E, pg, AF.Exp, [b_pg], [b_E])
                        at, b_at = atr.next()
                        kb.tt(at, E, sc[:, q // 2, :].unsqueeze(1).to_broadcast([128, 4, 128]), ALU.mult, [b_E, b_sc], [b_at])
                        segs[q] = (at, b_at)
                    seg(0)
                    for g_ in range(4):
                        seg(2 * g_ + 1)
                        py, b_py = pyr.next()
                        for qq in range(2):
                            q = 2 * g_ + qq
                            at, b_at = segs.pop(q)
                            for hh in range(4):
                                h = 4 * q + hh
                                kb.mm(py[:, h % 8, :], at[:, hh, :], vd[:, h, :], [b_at, b_vd], [b_py])
                            if qq == 0 and q + 2 < 8:
                                seg(q + 2)
                        pi, b_pi = pir.next()
                        kb.mm(pi, ct[:, g_, :], Sb[:, g_ * 512:(g_ + 1) * 512].rearrange("p (h d) -> p h d", d=64), [b_ct, b_Sb], [b_pi])
                        t1, b_t1 = t1r.next()
                        kb.tt(t1, pi, ecum[:, 8 * g_:8 * g_ + 8].unsqueeze(2).to_broadcast([128, 8, 64]), ALU.mult, [b_pi, b_ex], [b_t1])
                        kb.tt(yd[:, g_ * 512:(g_ + 1) * 512].rearrange("p (h d) -> p h d", d=64), py, t1, ALU.add, [b_py, b_t1], [b_yd])
                        yield
                    kb.tt(S, S, etot.unsqueeze(2).to_broadcast([128, 32, 64]), ALU.mult, [b_S, b_ex], [b_S], eng="pool")
                    for g_ in range(4):
                        pt, b_pt = pstr.next()
                        kb.mm(pt, kk[:, g_ * 128:(g_ + 1) * 128], vt[:, 8 * g_:8 * g_ + 8, :], [b_kk, b_vt], [b_pt])
                        Sg = S[:, 8 * g_:8 * g_ + 8, :]
                        kb.tt(Sg, Sg, pt.rearrange("p (h d) -> p h d", d=64), ALU.add, [b_S, b_pt], [b_S])
                    kb.copy(Sb.rearrange("p (h d) -> p h d", d=64), S, [b_S], [b_Sb], eng="act")
                    kb.dma(YD[rows, :], yd, [b_yd], [b_YD[c]])
                    yield
                stages = [A, B, B2_, C]
                n, S_ = len(order), 4
                for step in range(n + S_ - 1):
                    for si in range(S_):
                        i = step - si
                        if 0 <= i < n:
                            yield from stages[si](order[i])
            gens = [make_sweep("b"), make_sweep("f")]
            while gens:
                for g in list(gens):
                    try:
                        next(g)
                    except StopIteration:
                        gens.remove(g)
        P.barrier()
        phase_finish("ssd", j, ssd_w_out[j], skip_ctx=(L == 3))

    def phase_ret(L):
        TWO_PI = 2.0 * math.pi
        with ExitStack() as es:
            win, winb = load_w_groups(es, ret_w_in[0], D, [(i * 512, (i + 1) * 512) for i in range(12)], "rwin")
            mc = ModC(es, [0, 1])
            nr = norm_setup(es)
            cosC, b_cosC = kb.sb(es, [128, 64], F32, "cosC")
            sinC, b_sinC = kb.sb(es, [128, 64], F32, "sinC")
            cosR, b_cosR = kb.sb(es, [128, 32, 64], F32, "cosR")
            sinR, b_sinR = kb.sb(es, [128, 32, 64], F32, "sinR")
            with ExitStack() as es2:
                idxi, b_idxi = kb.sb(es2, [128, 64], I32, "idxi")
                kb.iota(idxi, [[1, 64]], 0, [b_idxi])
                inv, b_inv = kb.sb(es2, [128, 64], F32, "inv")
                kb.copy(inv, idxi, [b_idxi], [b_inv])
                kb.act(inv, inv, AF.Exp, [b_inv], [b_inv], scale=-math.log(10000.0) / 64.0)
                pi_, b_pi = kb.sb(es2, [128, 1], I32, "pi")
                kb.iota(pi_, [[0, 1]], 1, [b_pi])
                pf, b_pf = kb.sb(es2, [128, 4], F32, "pf")
                kb.copy(pf[:, 0:1], pi_, [b_pi], [b_pf])
                kb.ts(pf[:, 1:2], pf[:, 0:1], 64.0, None, ALU.is_ge, None, [b_pf], [b_pf])
                kb.stt(pf[:, 2:3], pf[:, 1:2], -64.0, pf[:, 0:1], ALU.mult, ALU.add, [b_pf], [b_pf])
                rowv, b_rowv = kb.sb(es2, [128, 32], F32, "rowv")
                ci, b_ci = kb.sb(es2, [128, 32], I32, "ci")
                kb.iota(ci, [[2, 32]], 0, [b_ci])
                kb.copy(rowv, ci, [b_ci], [b_rowv])
                kb.ts(rowv, rowv, pf[:, 1:2], None, ALU.add, None, [b_rowv, b_pf], [b_rowv])
                ang, b_ang = kb.sb(es2, [128, 33, 64], F32, "ang")
                kb.ts(ang[:, 32, :], inv, pf[:, 2:3], None, ALU.mult, None, [b_inv, b_pf], [b_ang])
                kb.tt(ang[:, 0:32, :], rowv.unsqueeze(2).to_broadcast([128, 32, 64]), inv.unsqueeze(1).to_broadcast([128, 32, 64]), ALU.mult,
                      [b_rowv, b_inv], [b_ang])
                red, b_red = kb.sb(es2, [128, 33, 64], F32, "red")
                ki, b_ki = kb.sb(es2, [128, 33, 64], I32, "ki")
                kf, b_kf = kb.sb(es2, [128, 33, 64], F32, "kf")
                for which, shift in (("sin", 0.0), ("cos", math.pi / 2)):
                    kb.ts(red, ang, shift, None, ALU.add, None, [b_ang], [b_red])
                    kb.ts(kf, red, 1.0 / TWO_PI, None, ALU.mult, None, [b_red], [b_kf])
                    kb.copy(ki, kf, [b_kf], [b_ki])
                    kb.copy(kf, ki, [b_ki], [b_kf])
                    kb.stt(red, kf, -TWO_PI, red, ALU.mult, ALU.add, [b_kf, b_red], [b_red])
                    kb.ts(kf, red, math.pi, -TWO_PI, ALU.is_gt, ALU.mult, [b_red], [b_kf])
                    kb.tt(red, red, kf, ALU.add, [b_red, b_kf], [b_red])
                    kb.ts(kf, red, -math.pi, TWO_PI, ALU.is_lt, ALU.mult, [b_red], [b_kf])
                    kb.tt(red, red, kf, ALU.add, [b_red, b_kf], [b_red])
                    kb.ts(red, red, math.pi, -math.pi, ALU.min, ALU.max, [b_red], [b_red])
                    dR, bR, dC, bC = (sinR, b_sinR, sinC, b_sinC) if which == "sin" else (cosR, b_cosR, cosC, b_cosC)
                    kb.act(dR, red[:, 0:32, :], AF.Sin, [b_red], [bR])
                    kb.act(dC, red[:, 32, :], AF.Sin, [b_red], [bC])
                P.barrier()
            csr = kb.sbring(es, [128, 2, 2, 64], F32, 2, "cs")
            for _ in range(2):
                cs_, b_cs_ = csr.next()
                kb.copy(cs_[:, 0, 1, :], cosC, [b_cosC], [b_cs_])
                kb.copy(cs_[:, 1, 1, :], sinC, [b_sinC], [b_cs_])
            xr = kb.sbring(es, [128, D], F32, 2, "rx")
            ar = kb.sbring(es, [128, D], BF16, 2, "ra")
            aTr = kb.sbring(es, [128, 8, 128], BF16, 2, "raT")
            tpr = kb.psring(es, [128, 8, 128], BF16, 2, "rtp")
            hr = kb.psring(es, [128, 512], F32, 4, "rh")
            qkr = kb.sbring(es, [128, 2, 1024], F32, 2, "rqk")
            qkb = kb.sbring(es, [128, 2, 1024], BF16, 3, "rqkb")
            tA = kb.sbring(es, [128, 8, 64], F32, 2, "rtA")
            tB = kb.sbring(es, [128, 8, 64], F32, 2, "rtB")
            tTr = kb.sbring(es, [128, 8, 128], BF16, 2, "rtT")
            vbr = kb.sbring(es, [128, 2048], BF16, 1, "rvb")
            gbr = kb.sbring(es, [128, 2048], BF16, 2, "rgb")
            st = {}

            def N_(c):
                kd = kind_of(c)
                xc, b_xc = xr.next()
                kb.dma(xc, X[c * 128:(c + 1) * 128, :], [b_X[c]], [b_xc])
                st[("a", c)] = norm_a(nr, xc, b_xc, *mc.get(kd, 0), *mc.get(kd, 1), ar)

            def A(c):
                a, b_a = st.pop(("a", c))
                aT, b_aT = aTr.next()
                trans_a(a, b_a, tpr, aT, b_aT)
                st[c] = (aT, b_aT)

            def B(c):
                rows = slice(c * 128, (c + 1) * 128)
                aT, b_aT = st.pop(c)
                qk, b_qk = qkr.next()
                vb, b_vb = vbr.next()
                gb, b_gb = gbr.next()
                for gi in range(12):
                    h, b_h = hr.next()
                    for k in range(8):
                        kb.mm(h, aT[:, k, :], win[:, k, gi * 512:(gi + 1) * 512], [b_aT, winb[gi]], [b_h], start=(k == 0), stop=(k == 7))
                    if gi < 2:
                        kb.copy(qk[:, 0, gi * 512:(gi + 1) * 512], h, [b_h], [b_qk], eng="act")
                    elif gi < 4:
                        kb.act(qk[:, 1, (gi - 2) * 512:(gi - 1) * 512], h, AF.Copy, [b_h], [b_qk], scale=1.0 / 16.0)
                    elif gi < 8:
                        kb.copy(vb[:, (gi - 4) * 512:(gi - 3) * 512], h, [b_h], [b_vb], eng="dve")
                    else:
                        kb.act(gb[:, (gi - 8) * 512:(gi - 7) * 512], h, AF.Silu, [b_h], [b_gb])
                kb.dma(VV[rows, :], vb, [b_vb], [b_VV[c]])
                kb.dma(GS[rows, :], gb, [b_gb], [b_GS[c]])
                st[("q", c)] = (qk, b_qk)

            def C(c):
                rows = slice(c * 128, (c + 1) * 128)
                qk, b_qk = st.pop(("q", c))
                qb, b_qb = qkb.next()
                if c < 2:
                    kb.copy(qb, qk, [b_qk], [b_qb], eng="pool")
                else:
                    cs_, b_cs_ = csr.next()
                    kb.copy(cs_[:, 0, 0, :], cosR[:, c - 2, :], [b_cosR], [b_cs_], eng="pool")
                    kb.copy(cs_[:, 1, 0, :], sinR[:, c - 2, :], [b_sinR], [b_cs_], eng="pool")
                    for w_ in range(2):
                        v6 = qk[:, w_, :].rearrange("p (a h d) -> p a h d", h=2, d=64)
                        o6 = qb[:, w_, :].rearrange("p (a h d) -> p a h d", h=2, d=64)
                        t1_, t2_ = v6[:, :, 0, :], v6[:, :, 1, :]
                        csb = cs_[:, 0, :, :].unsqueeze(1).to_broadcast([128, 4, 2, 64])
                        snb = cs_[:, 1, :, :].unsqueeze(1).to_broadcast([128, 4, 2, 64])

                        def v4(ap):
                            return ap.rearrange("p (a b) d -> p a b d", b=2)
                        A_, b_A_ = tA.next()
                        B_, b_B_ = tB.next()
                        eng1 = "dve" if w_ == 0 else "pool"
                        kb.tt(v4(A_), v4(t1_), csb, ALU.mult, [b_qk, b_cs_], [b_A_], eng=eng1)
                        kb.tt(v4(B_), v4(t2_), snb, ALU.mult, [b_qk, b_cs_], [b_B_], eng=eng1)
                        kb.tt(o6[:, :, 0, :], A_, B_, ALU.subtract, [b_A_, b_B_], [b_qb], eng=eng1)
                        A2, b_A2 = tA.next()
                        B2, b_B2 = tB.next()
                        kb.tt(v4(A2), v4(t2_), csb, ALU.mult, [b_qk, b_cs_], [b_A2], eng=eng1)
                        kb.tt(v4(B2), v4(t1_), snb, ALU.mult, [b_qk, b_cs_], [b_B2], eng=eng1)
                        kb.tt(o6[:, :, 1, :], A2, B2, ALU.add, [b_A2, b_B2], [b_qb], eng=eng1)
                st[("qb", c)] = (qb, b_qb)

            def C2(c):
                rows = slice(c * 128, (c + 1) * 128)
                qb, b_qb = st.pop(("qb", c))
                kb.dma(KB_[rows, :], qb[:, 1, :], [b_qb], [b_KB[c]])
                for w_, (dst, bd) in enumerate(((QT, b_QT[c]), (KT, b_KT[c]))):
                    tp, b_tp = tpr.next()
                    for q in range(8):
                        kb.tr(tp[:, q, :], qb[:, w_, q * 128:(q + 1) * 128], [b_qb], [b_tp])
                    tT, b_tT = tTr.next()
                    kb.copy(tT, tp, [b_tp], [b_tT], eng="act" if w_ else "dve")
                    kb.dma(dst[c], tT, [b_tT], [bd])
            pipeline(list(range(NCH)), [N_, A, B, C, C2])
        P.barrier()
        with ExitStack() as es:
            pyr = kb.psring(es, [128, 512], F32, 2, "wpy")
            pir = kb.psring(es, [128, 512], F32, 2, "wpi")
            pstr = kb.psring(es, [128, 512], F32, 2, "wpst")
            consts = {}
            for sweep in ("b", "f"):
                fin = sweep == "f"
                M = msk["M" + sweep]
                sgn = 1.0 if fin else -1.0
                lg, b_lg = kb.sb(es, [128, 4], F32, "lg")
                kb.bcast_load(lg, b_lg, ret_decay[0, (0 if fin else 4):(4 if fin else 8)])
                kb.act(lg, lg, AF.Exp, [b_lg], [b_lg])
                kb.ts(lg, lg, -1.0, None, ALU.mult, None, [b_lg], [b_lg])
                LT, b_LT = kb.sb(es, [128, 4, 128], BF16, "LT")
                vec, b_vec = kb.sb(es, [128, 3, 4], F32, "vec")
                with ExitStack() as es2:
                    Di, b_Di = kb.sb(es2, [128, 128], I32, "Di")
                    kb.iota(Di, [[1, 128]], -1, [b_Di])
                    Df, b_Df = kb.sb(es2, [128, 128], F32, "Df")
                    kb.copy(Df, Di, [b_Di], [b_Df])
                    arg, b_arg = kb.sb(es2, [128, 128], F32, "arg")
                    lgs, b_lgs = kb.sb(es2, [128, 4], F32, "lgs")
                    kb.ts(lgs, lg, sgn, None, ALU.mult, None, [b_lg], [b_lgs])
                    for h in range(4):
                        kb.stt(arg, Df, lgs[:, h:h + 1], M[0], ALU.mult, ALU.add, [b_Df, b_lgs, M[1]], [b_arg])
                        kb.act(LT[:, h, :], arg, AF.Exp, [b_arg], [b_LT])
                    pi_, b_pi = kb.sb(es2, [128, 1], I32, "pi2")
                    kb.iota(pi_, [[0, 1]], 1, [b_pi])
                    pf, b_pf = kb.sb(es2, [128, 3], F32, "pf2")
                    kb.copy(pf[:, 0:1], pi_, [b_pi], [b_pf])
                    if fin:
                        kb.ts(pf[:, 1:2], pf[:, 0:1], 1.0, None, ALU.add, None, [b_pf], [b_pf])
                        kb.ts(pf[:, 2:3], pf[:, 0:1], -1.0, 127.0, ALU.mult, ALU.add, [b_pf], [b_pf])
                    else:
                        kb.ts(pf[:, 1:2], pf[:, 0:1], -1.0, 128.0, ALU.mult, ALU.add, [b_pf], [b_pf])
                        kb.copy(pf[:, 2:3], pf[:, 0:1], [b_pf], [b_pf])
                    kb.ts(vec[:, 0, :], lg, pf[:, 1:2], None, ALU.mult, None, [b_lg, b_pf], [b_vec])
                    kb.ts(vec[:, 1, :], lg, pf[:, 2:3], None, ALU.mult, None, [b_lg, b_pf], [b_vec])
                    kb.ts(vec[:, 2, :], lg, 128.0, None, ALU.mult, None, [b_lg], [b_vec])
                    kb.act(vec, vec, AF.Exp, [b_vec], [b_vec])
                    P.barrier()
                consts[sweep] = (LT, b_LT, vec, b_vec)

            def make_sweep(sweep):
                fin = sweep == "f"
                LT, b_LT, vec, b_vec = consts[sweep]
                YD, b_YD = (YF, b_YF) if fin else (YB, b_YB)
                S, b_S = kb.sb(es, [128, 8, 512], F32, "RS")
                Sb, b_Sb = kb.sb(es, [128, 8, 512], BF16, "RSb")
                kb.memset(S, 0.0, [b_S])
                kb.memset(Sb, 0.0, [b_Sb])
                qtr = kb.sbring(es, [128, 8, 128], BF16, 3, "wqt")
                ktr = kb.sbring(es, [128, 8, 128], BF16, 3, "wkt")
                kkr = kb.sbring(es, [128, 1024], BF16, 3, "wk")
                vr = kb.sbring(es, [128, 4, 512], BF16, 3, "wv")
                vtr = kb.sbring(es, [128, 4, 512], BF16, 2, "wvt")
                atr = kb.sbring(es, [128, 4, 128], BF16, 2, "wat")
                t1r = kb.sbring(es, [128, 512], F32, 2, "wt1")
                ydr = kb.sbring(es, [128, 2048], BF16, 2, "wyd")
                psc = kb.psring(es, [128, 4, 128], F32, 1, "wpsc")
                order = list(range(NCH)) if fin else [1, 0] + list(range(NCH - 1, 1, -1))
                st = {}

                def A(c):
                    rows = slice(c * 128, (c + 1) * 128)
                    qt, b_qt = qtr.next()
                    kb.dma(qt, QT[c], [b_QT[c]], [b_qt])
                    kt, b_kt = ktr.next()
                    kb.dma(kt, KT[c], [b_KT[c]], [b_kt])
                    kk, b_kk = kkr.next()
                    kb.dma(kk, KB_[rows, :], [b_KB[c]], [b_kk])
                    v, b_v = vr.next()
                    kb.dma(v, VV[rows, :].rearrange("p (h d) -> p h d", d=512), [b_VV[c]], [b_v])
                    st[c] = dict(qt=qt, b_qt=b_qt, kt=kt, b_kt=b_kt, kk=kk, b_kk=b_kk, v=v, b_v=b_v)
                    yield

                def B(c):
                    s_ = st[c]
                    qt, b_qt, kt, b_kt, v, b_v = s_["qt"], s_["b_qt"], s_["kt"], s_["b_kt"], s_["v"], s_["b_v"]
                    ps_, b_ps = psc.next()
                    for h in range(4):
                        for n_ in range(2):
                            kb.mm(ps_[:, h, :], kt[:, 2 * h + n_, :], qt[:, 2 * h + n_, :], [b_kt, b_qt], [b_ps], start=(n_ == 0), stop=(n_ == 1))
                    at, b_at = atr.next()
                    kb.tt(at, ps_, LT, ALU.mult, [b_ps, b_LT], [b_at])
                    vt, b_vt = vtr.next()
                    kb.tt(vt, v, vec[:, 1, :].unsqueeze(2).to_broadcast([128, 4, 512]), ALU.mult, [b_v, b_vec], [b_vt], eng="pool")
                    s_.update(at=at, b_at=b_at, vt=vt, b_vt=b_vt)
                    yield

                def C(c):
                    s_ = st.pop(c)
                    rows = slice(c * 128, (c + 1) * 128)
                    qt, b_qt, kk, b_kk, v, b_v = s_["qt"], s_["b_qt"], s_["kk"], s_["b_kk"], s_["v"], s_["b_v"]
                    at, b_at, vt, b_vt = s_["at"], s_["b_at"], s_["vt"], s_["b_vt"]
                    yd, b_yd = ydr.next()
                    for h in range(4):
                        py, b_py = pyr.next()
                        kb.mm(py, at[:, h, :], v[:, h, :], [b_at, b_v], [b_py])
                        pi2, b_pi2 = pir.next()
                        for n_ in range(2):
                            kb.mm(pi2, qt[:, 2 * h + n_, :], Sb[:, 2 * h + n_, :], [b_qt, b_Sb], [b_pi2], start=(n_ == 0), stop=(n_ == 1))
                        t1, b_t1 = t1r.next()
                        kb.act(t1, pi2, AF.Identity, [b_pi2, b_vec], [b_t1], scale=vec[:, 0, h:h + 1])
                        kb.tt(yd[:, h * 512:(h + 1) * 512], py, t1, ALU.add, [b_py, b_t1], [b_yd])
                        if h % 2 == 1:
                            yield
                    for h in range(4):
                        for n_ in range(2):
                            pt, b_pt = pstr.next()
                            kb.mm(pt, kk[:, h * 256 + n_ * 128:h * 256 + (n_ + 1) * 128], vt[:, h, :], [b_kk, b_vt], [b_pt])
                            Sg = S[:, 2 * h + n_, :]
                            kb.stt(Sg, Sg, vec[:, 2, h:h + 1], pt, ALU.mult, ALU.add, [b_S, b_vec, b_pt], [b_S])
                    kb.copy(Sb, S, [b_S], [b_Sb], eng="act")
                    kb.dma(YD[rows, :], yd, [b_yd], [b_YD[c]])
                    yield
                stages = [A, B, C]
                n, S_ = len(order), 3
                for step in range(n + S_ - 1):
                    for si in range(S_):
                        i = step - si
                        if 0 <= i < n:
                            yield from stages[si](order[i])
            gens = [make_sweep("b"), make_sweep("f")]
            while gens:
                for g in list(gens):
                    try:
                        next(g)
                    except StopIteration:
                        gens.remove(g)
        P.barrier()
        phase_finish("ret", 0, ret_w_out[0])

    def phase_final(raw=False):
        with ExitStack() as es:
            nr = norm_setup(es, tmp=False, nss=4)
            fg, b_fg = kb.sb(es, [128, D], F32, "fg")
            kb.bcast_load(fg, b_fg, final_g)
            xr = kb.sbring(es, [128, D], F32, 4, "zx")
            orr = kb.sbring(es, [128, D], F32, 3, "zo")
            st = {}

            def A(c):
                xc, b_xc = xr.next()
                kb.dma(xc, X[c * 128:(c + 1) * 128, :], [b_X[c]], [b_xc])
                st[c] = (xc, b_xc)

            def B(c):
                xc, b_xc = st.pop(c)
                if raw:
                    kb.dma(out[(c - 2) * 128:(c - 1) * 128, :], xc, [b_xc], [b_out[c]])
                    return
                ss, b_ss = rstd_of(nr, xc, b_xc, D)
                o, b_o = orr.next()
                kb.stt(o, xc, ss, fg, ALU.mult, ALU.mult, [b_xc, b_ss, b_fg], [b_o])
                kb.dma(out[(c - 2) * 128:(c - 1) * 128, :], o, [b_o], [b_out[c]])
            pipeline(list(range(2, NCH)), [A, lambda c: None, B])
        P.barrier()

    kinds = [0, 1, 2, 0]
    for L in range(debug_layers):
        phase_mod(L)
        if kinds[L] == 0:
            phase_ssd(L, L // 3)
        elif kinds[L] == 1:
            phase_gmlp(L)
        else:
            phase_ret(L)
        if not (skip_last_ffn and L == debug_layers - 1):
            phase_ffn(L)
    phase_final(raw=debug_raw)
    P.op("sp", None, reads=b_out.all())
    P.emit()
    glob.close()
    return nc


def make_in_maps(inputs, cores):
    f = lambda a: np.ascontiguousarray(np.asarray(a, dtype=np.float32))
    shared = {
        "mod_w": f(inputs["mod_w"]), "mod_b": f(inputs["mod_b"]), "norm1_g": f(inputs["norm1_g"]), "norm2_g": f(inputs["norm2_g"]),
        "ffn_w1": f(inputs["ffn_w1"]), "ffn_w3": f(inputs["ffn_w3"]), "ffn_w2": f(inputs["ffn_w2"]),
        "ssd_w_in": f(inputs["ssd_w_in"]), "ssd_conv_w": f(inputs["ssd_conv_w"]), "ssd_conv_b": f(inputs["ssd_conv_b"]),
        "ssd_alog": f(np.concatenate([inputs["ssd_a_log_f"], inputs["ssd_a_log_b"]], axis=1)),
        "ssd_dtb": f(np.concatenate([inputs["ssd_dt_bias_f"], inputs["ssd_dt_bias_b"]], axis=1)),
        "ssd_d": f(inputs["ssd_d"]), "ssd_norm_g": f(inputs["ssd_norm_g"]), "ssd_w_out": f(inputs["ssd_w_out"]),
        "gmlp_w_in": f(inputs["gmlp_w_in"]), "gmlp_norm_g": f(inputs["gmlp_norm_g"]), "gmlp_w_s": f(inputs["gmlp_w_s"]),
        "gmlp_b_s": f(inputs["gmlp_b_s"]), "gmlp_w_out": f(inputs["gmlp_w_out"]),
        "ret_w_in": f(inputs["ret_w_in"]),
        "ret_decay": f(np.concatenate([inputs["ret_decay_f"], inputs["ret_decay_b"]], axis=1)),
        "ret_w_out": f(inputs["ret_w_out"]), "final_g": f(inputs["final_g"]),
    }
    maps = []
    for b in cores:
        m = dict(shared)
        m["xin"] = f(np.concatenate([inputs["ctx"][b], inputs["x"][b]], axis=0))
        m["cc"] = f(np.stack([inputs["c"][b], inputs["c_ctx"]], axis=0))
        maps.append(m)
    return maps


def kernel(**inputs):
    nc = build_program()
    B = inputs["x"].shape[0]
    maps = make_in_maps(inputs, list(range(B)))
    res = run_bass_kernel_spmd(nc, maps, core_ids=list(range(B)))
    return np.stack([np.asarray(r["out"], dtype=np.float32) for r in res.results], axis=0)
```

```python
import math
from contextlib import ExitStack
import numpy as np
import concourse.bass as bass
import concourse.mybir as mybir
from concourse.alu_op_type import AluOpType as ALU
from concourse.bass_utils import run_bass_kernel_spmd

AF = mybir.ActivationFunctionType
F32 = mybir.dt.float32
BF16 = mybir.dt.bfloat16
I32 = mybir.dt.int32

D = 1024
SEQ = 4096
CTX = 256
NCH = (SEQ + CTX) // 128
NT = NCH * 128
DFF = 2816
EPS = 1e-6
NEG = -1.0e30
N_CORES = 4
SEM_ROT = 6000


class Buf:
    __slots__ = ("name", "w", "rs", "grp")

    def __init__(self, name="", grp=None):
        self.name = name
        self.w = None
        self.rs = []
        self.grp = grp


class BufMap:
    def __init__(self, name):
        self.name = name
        self.d = {}

    def __getitem__(self, k):
        if k not in self.d:
            self.d[k] = Buf(f"{self.name}{k}", grp=self.name)
        return self.d[k]

    def all(self):
        return list(self.d.values())


class Op:
    __slots__ = ("eng", "fn", "deps", "signal", "is_dma", "key", "sem", "cnt", "ninc", "phase")


class Prog:
    ENGS = ("pe", "dve", "act", "pool", "sp")

    def __init__(self, nc):
        self.nc = nc
        self.ops = {e: [] for e in self.ENGS}
        self.slot_cnt = []
        self.slot_gen = []
        self.gen_final = {}
        self.key2slot = {}
        self.touched = {}
        self.phase = 0

    def op(self, eng, fn, reads=(), writes=(), dma=False, key=None, ninc=1, extra=()):
        o = Op()
        o.eng, o.fn, o.is_dma, o.signal, o.ninc = eng, fn, dma, dma, ninc
        o.sem = None
        o.cnt = None
        o.key = None
        o.phase = self.phase
        deps = list(extra)
        for b in reads:
            if b.w is not None:
                deps.append(b.w)
            self.touched[id(b)] = b
        for b in writes:
            if b.w is not None:
                deps.append(b.w)
            deps.extend(b.rs)
            self.touched[id(b)] = b
        dd, seen = [], set()
        for d in deps:
            if id(d) in seen or d is o:
                continue
            seen.add(id(d))
            if (not d.is_dma) and d.eng == eng and (not dma) and eng == "pe":
                continue
            req = None
            if d.is_dma:
                si, gen = d.key
                req = self.slot_cnt[si] if self.slot_gen[si] == gen else self.gen_final[(si, gen)]
            dd.append((d, req))
        o.deps = dd
        if dma:
            if key is None:
                key = (writes[0].grp or id(writes[0])) if writes else id(o)
            if key not in self.key2slot:
                self.key2slot[key] = len(self.key2slot)
                if len(self.key2slot) > len(self.slot_cnt):
                    self.slot_cnt.append(0)
                    self.slot_gen.append(0)
            si = self.key2slot[key]
            self.slot_cnt[si] += 16 * ninc
            o.key = (si, self.slot_gen[si])
            o.cnt = self.slot_cnt[si]
        for b in reads:
            b.rs.append(o)
        for b in writes:
            b.w = o
            b.rs = []
        self.ops[eng].append(o)
        return o

    def barrier(self):
        fin, seen = [], set()
        for b in self.touched.values():
            for o in ([b.w] if b.w is not None else []) + list(b.rs):
                if id(o) not in seen:
                    seen.add(id(o))
                    fin.append(o)
        for e in self.ENGS:
            self.op(e, None, extra=fin)
        self.touched = {}
        self.key2slot = {}
        self.phase += 1
        for i in range(len(self.slot_cnt)):
            if self.slot_cnt[i] > SEM_ROT:
                self.gen_final[(i, self.slot_gen[i])] = self.slot_cnt[i]
                self.slot_cnt[i] = 0
                self.slot_gen[i] += 1

    def emit(self):
        nc = self.nc
        for e in self.ENGS:
            for o in self.ops[e]:
                for d, _ in o.deps:
                    d.signal = True
        dsem = {}
        nes = 0
        for e in self.ENGS:
            c = 0
            cur = None
            ph = -1
            for o in self.ops[e]:
                if o.is_dma:
                    if o.key not in dsem:
                        dsem[o.key] = nc.alloc_semaphore(name=f"d{len(dsem)}")
                    o.sem = dsem[o.key]
                elif o.signal:
                    if cur is None or (o.phase != ph and c > SEM_ROT):
                        cur = nc.alloc_semaphore(name=f"s_{e}{nes}")
                        nes += 1
                        c = 0
                    ph = o.phase
                    c += 1
                    o.sem = cur
                    o.cnt = c
        print("semaphores used:", nes, "+", len(dsem), flush=True)

        def run(e, engobj):
            waited = {}
            for o in self.ops[e]:
                for d, req in o.deps:
                    if d.sem is None:
                        continue
                    cnt = d.cnt if req is None else req
                    k = id(d.sem)
                    if waited.get(k, 0) >= cnt:
                        continue
                    waited[k] = cnt
                    engobj.wait_ge(d.sem, cnt)
                if o.fn is None:
                    continue
                r = o.fn(engobj)
                if o.signal:
                    if o.is_dma:
                        if not isinstance(r, (list, tuple)):
                            r = [r]
                        assert len(r) == o.ninc, (len(r), o.ninc)
                        for ins in r:
                            ins.then_inc(o.sem, 16)
                    else:
                        if isinstance(r, (list, tuple)):
                            r = r[-1]
                        r.then_inc(o.sem, 1)

        with nc.Block() as block:
            @block.sync
            def _(eng):
                run("sp", eng)

            @block.tensor
            def _(eng):
                run("pe", eng)

            @block.vector
            def _(eng):
                run("dve", eng)

            @block.scalar
            def _(eng):
                run("act", eng)

            @block.gpsimd
            def _(eng):
                run("pool", eng)


class Ring:
    def __init__(self, tiles):
        self.t = tiles
        self.i = 0

    def next(self):
        r = self.t[self.i % len(self.t)]
        self.i += 1
        return r


class K:
    def __init__(self, nc):
        self.nc = nc
        self.P = Prog(nc)
        self.uid = 0

    def sb(self, es, shape, dt, name="t"):
        self.uid += 1
        t = es.enter_context(self.nc.sbuf_tensor(f"{name}{self.uid}", list(shape), dt))
        return t.ap(), Buf(name)

    def ps(self, es, shape, dt=F32, name="p"):
        self.uid += 1
        t = es.enter_context(self.nc.psum_tensor(f"{name}{self.uid}", list(shape), dt))
        return t.ap(), Buf(name)

    def sbring(self, es, shape, dt, n, name="r"):
        return Ring([self.sb(es, shape, dt, name) for _ in range(n)])

    def psring(self, es, shape, dt, n, name="pr"):
        return Ring([self.ps(es, shape, dt, name) for _ in range(n)])

    def dram(self, shape, dt, name):
        if getattr(self, "dbg_scratch", False):
            return self.nc.dram_tensor(name, list(shape), dt, kind="ExternalOutput").ap(), BufMap(name)
        return self.nc.dram_tensor(name, list(shape), dt).ap(), BufMap(name)

    def dma(self, out, in_, rd, wr, eng="sp", **kw):
        return self.P.op(eng, lambda e: e.dma_start(out=out, in_=in_, **kw), reads=rd, writes=wr, dma=True)

    def mm(self, out, lhsT, rhs, rd, wr, start=True, stop=True):
        return self.P.op("pe", lambda e: e.matmul(out, lhsT=lhsT, rhs=rhs, start=start, stop=stop), reads=rd, writes=wr)

    def tr(self, out, in_, rd, wr):
        ident = self.ident
        return self.P.op("pe", lambda e: e.transpose(out=out, in_=in_, identity=ident), reads=rd + [self.b_ident], writes=wr)

    def act(self, out, in_, func, rd, wr, **kw):
        return self.P.op("act", lambda e: e.activation(out=out, in_=in_, func=func, **kw), reads=rd, writes=wr)

    def tt(self, out, in0, in1, op, rd, wr, eng="dve"):
        return self.P.op(eng, lambda e: e.tensor_tensor(out=out, in0=in0, in1=in1, op=op), reads=rd, writes=wr)

    def ts(self, out, in0, s1, s2, op0, op1, rd, wr, eng="dve"):
        if op1 is None:
            return self.P.op(eng, lambda e: e.tensor_scalar(out=out, in0=in0, scalar1=s1, scalar2=None, op0=op0), reads=rd, writes=wr)
        return self.P.op(eng, lambda e: e.tensor_scalar(out=out, in0=in0, scalar1=s1, scalar2=s2, op0=op0, op1=op1), reads=rd, writes=wr)

    def stt(self, out, in0, scalar, in1, op0, op1, rd, wr):
        return self.P.op("dve", lambda e: e.scalar_tensor_tensor(out=out, in0=in0, scalar=scalar, in1=in1, op0=op0, op1=op1), reads=rd, writes=wr)

    def copy(self, out, in_, rd, wr, eng="dve"):
        if eng == "act":
            return self.P.op("act", lambda e: e.copy(out=out, in_=in_), reads=rd, writes=wr)
        return self.P.op(eng, lambda e: e.tensor_copy(out=out, in_=in_), reads=rd, writes=wr)

    def iota(self, ap, pattern, cm, wr):
        return self.P.op("pool", lambda e: e.iota(ap, pattern=pattern, base=0, channel_multiplier=cm), writes=wr)

    def memset(self, ap, val, wr, eng="pool"):
        return self.P.op(eng, lambda e: e.memset(ap, val), writes=wr)

    def asel(self, ap, buf, step, cm, cmp, fill):
        return self.P.op("pool", lambda e: e.affine_select(out=ap, in_=ap, pattern=[[step, ap.shape[-1]]], compare_op=cmp,
                                                            fill=fill, base=0, channel_multiplier=cm), reads=[buf], writes=[buf])

    def load_w(self, dst, b_dst, src, K_, c0=None, c1=None):
        nk = K_ // 128
        s = src if c0 is None else src[:, c0:c1]

        def fn(e):
            return e.dma_start(out=dst, in_=s.rearrange("(k p) f -> p k f", p=128))
        return self.P.op("pool", fn, writes=[b_dst], dma=True)

    def bcast_load(self, dst, b_dst, row_ap):
        return self.dma(dst, row_ap.partition_broadcast(128), [], [b_dst])


def pipeline(items, stages):
    n, S = len(items), len(stages)
    for step in range(n + S - 1):
        for si in range(S):
            i = step - si
            if 0 <= i < n:
                stages[si](items[i])


def build_program(debug_layers=4, debug_raw=False, skip_last_ffn=False, dbg_scratch=False):
    nc = bass.Bass("TRN2", target_bir_lowering=False)
    kb = K(nc)
    kb.dbg_scratch = dbg_scratch
    P = kb.P

    def din(name, shape):
        return nc.dram_tensor(name, list(shape), F32, kind="ExternalInput").ap()

    xin = din("xin", [NT, D])
    cc = din("cc", [2, D])
    mod_w = din("mod_w", [4, D, 6 * D])
    mod_b = din("mod_b", [4, 6 * D])
    norm1_g = din("norm1_g", [4, D])
    norm2_g = din("norm2_g", [4, D])
    ffn_w1 = din("ffn_w1", [4, D, DFF])
    ffn_w3 = din("ffn_w3", [4, D, DFF])
    ffn_w2 = din("ffn_w2", [4, DFF, D])
    ssd_w_in = din("ssd_w_in", [2, D, 5184])
    ssd_conv_w = din("ssd_conv_w", [2, 5, 3072])
    ssd_conv_b = din("ssd_conv_b", [2, 3072])
    ssd_alog = din("ssd_alog", [2, 64])
    ssd_dtb = din("ssd_dtb", [2, 64])
    ssd_d = din("ssd_d", [2, 32])
    ssd_norm_g = din("ssd_norm_g", [2, 2048])
    ssd_w_out = din("ssd_w_out", [2, 2048, D])
    gmlp_w_in = din("gmlp_w_in", [1, D, 4096])
    gmlp_norm_g = din("gmlp_norm_g", [1, 2048])
    gmlp_w_s = din("gmlp_w_s", [1, 8, 128, 128])
    gmlp_b_s = din("gmlp_b_s", [1, 8, 128])
    gmlp_w_out = din("gmlp_w_out", [1, 2048, D])
    ret_w_in = din("ret_w_in", [1, D, 6144])
    ret_decay = din("ret_decay", [1, 8])
    ret_w_out = din("ret_w_out", [1, 2048, D])
    final_g = din("final_g", [D])
    out = nc.dram_tensor("out", [SEQ, D], F32, kind="ExternalOutput").ap()
    b_out = BufMap("out")

    X, b_X = kb.dram([NT, D], F32, "X")
    MODR, b_MODR = kb.dram([2, 6, D], F32, "MODR")
    RAWW = 4 + CTX + 4 + SEQ + 4
    RAWT, b_RAWT = kb.dram([3072, RAWW], BF16, "RAWT")
    COL = [4, 4 + CTX + 4]
    ZS, b_ZS = kb.dram([NT, 2048], BF16, "ZS")
    DT, b_DT = kb.dram([NT, 64], F32, "DT")
    LA, b_LA = kb.dram([NT, 64], F32, "LA")
    XS, b_XS = kb.dram([NT, 2048], BF16, "XS")
    KB_, b_KB = kb.dram([NT, 1024], BF16, "KB")
    BT, b_BT = kb.dram([4, 128, NT], BF16, "BT")
    CT, b_CT = kb.dram([4, 128, NT], BF16, "CT")
    YB, b_YB = kb.dram([NT, 2048], BF16, "YB")
    YF, b_YF = kb.dram([NT, 2048], BF16, "YF")
    QT, b_QT = kb.dram([NCH, 128, 8, 128], BF16, "QT")
    KT, b_KT = kb.dram([NCH, 128, 8, 128], BF16, "KT")
    VV, b_VV = kb.dram([NT, 2048], BF16, "VV")
    GS, b_GS = kb.dram([NT, 2048], BF16, "GS")

    glob = ExitStack()
    identf, b_identf = kb.sb(glob, [128, 128], F32, "identf")
    kb.ident, kb.b_ident = kb.sb(glob, [128, 128], BF16, "ident")
    ones, b_ones = kb.sb(glob, [128, 128], BF16, "ones")
    tmpc, b_tmpc = kb.sb(glob, [128, 128], F32, "tmpc")
    negh, b_negh = kb.sb(glob, [128, 4], F32, "negh")
    msk = {}
    kb.memset(identf, 0.0, [b_identf])
    kb.asel(identf, b_identf, -1, 1, ALU.not_equal, 1.0)
    kb.copy(kb.ident, identf, [b_identf], [kb.b_ident])
    kb.memset(tmpc, 1.0, [b_tmpc])
    kb.copy(ones, tmpc, [b_tmpc], [b_ones])
    kb.memset(negh, -0.5, [b_negh])
    specs = {
        "Uf": (1, -1, ALU.is_ge, 1.0, 0.0), "SLf": (-1, 1, ALU.is_gt, 1.0, 0.0), "Mf": (1, -1, ALU.is_ge, 0.0, NEG),
        "Ub": (-1, 1, ALU.is_ge, 1.0, 0.0), "SLb": (1, -1, ALU.is_gt, 1.0, 0.0), "Mb": (-1, 1, ALU.is_ge, 0.0, NEG),
    }
    for nm, (st, cm, cmp, base, fill) in specs.items():
        f32t, b_f = kb.sb(glob, [128, 128], F32, nm + "f")
        kb.memset(f32t, base, [b_f])
        kb.asel(f32t, b_f, st, cm, cmp, fill)
        if nm[0] == "M":
            t4, b_4 = kb.sb(glob, [128, 4, 128], BF16, nm + "4")
            for q in range(4):
                kb.copy(t4[:, q, :], f32t, [b_f], [b_4])
            msk[nm] = (f32t, b_f, t4, b_4)
        else:
            tb, b_b = kb.sb(glob, [128, 128], BF16, nm + "b")
            kb.copy(tb, f32t, [b_f], [b_b])
            msk[nm] = (f32t, b_f, tb, b_b)

    for c in range(0, NCH, 2):
        kb.dma(X[c * 128:(c + 2) * 128, :], xin[c * 128:(c + 2) * 128, :], [], [b_X[c], b_X[c + 1]])
    P.barrier()

    def norm_setup(es, width=D, tmp=True, nss=3):
        r = {}
        r["ss"] = kb.sbring(es, [128, 1], F32, nss, "ss")
        r["junk"] = kb.sb(es, [128, width], BF16, "junk")
        if tmp:
            r["tmp"] = kb.sbring(es, [128, D], F32, 1, "ntmp")
        return r

    def rstd_of(r, src, b_src, width, nseg=1):
        ss, b_ss = r["ss"].next() if nseg == 1 else r["ss4"].next()
        junk, b_junk = r["junk"]
        for s in range(nseg):
            kb.act(junk[:, 0:width], src[:, s * width:(s + 1) * width], AF.Square, [b_src], [b_junk, b_ss], accum_out=ss[:, s:s + 1])
        kb.ts(ss, ss, 1.0 / width, EPS, ALU.mult, ALU.add, [b_ss], [b_ss])
        kb.tt(ss, ss, negh[:, 0:nseg], ALU.pow, [b_ss, b_negh], [b_ss], eng="pool")
        return ss, b_ss

    def norm_a(r, xc, b_xc, A, b_A, Bc, b_B, a_ring):
        ss, b_ss = rstd_of(r, xc, b_xc, D)
        tmp, b_tmp = r["tmp"].next()
        kb.stt(tmp, xc, ss, A, ALU.mult, ALU.mult, [b_xc, b_ss, b_A], [b_tmp])
        a, b_a = a_ring.next()
        kb.tt(a, tmp, Bc, ALU.add, [b_tmp, b_B], [b_a], eng="pool")
        return a, b_a

    def trans_a(a, b_a, tp_ring, dst, b_dst):
        tp, b_tp = tp_ring.next()
        for k in range(8):
            kb.tr(tp[:, k, :], a[:, k * 128:(k + 1) * 128], [b_a], [b_tp])
        kb.copy(dst, tp, [b_tp], [b_dst], eng="act")

    def norm_mod_T(r, xc, b_xc, A, b_A, Bc, b_B, a_ring, tp_ring, dst, b_dst):
        a, b_a = norm_a(r, xc, b_xc, A, b_A, Bc, b_B, a_ring)
        trans_a(a, b_a, tp_ring, dst, b_dst)

    class ModC:
        def __init__(self, es, idxs):
            self.idxs = idxs
            self.t = {i: kb.sb(es, [128, D], F32, "modc") for i in idxs}
            self.cur = None

        def get(self, kind, i):
            if kind != self.cur:
                self.cur = kind
                for ii in self.idxs:
                    t, b = self.t[ii]
                    kb.dma(t, MODR[kind, ii, :].partition_broadcast(128), [b_MODR[0]], [b])
            return self.t[i]

    def kind_of(c):
        return 1 if c < 2 else 0

    def load_w_groups(es, src, K_, groups, name):
        F_ = src.shape[1]
        nk = K_ // 128
        w, _ = kb.sb(es, [128, nk, F_], BF16, name)
        bufs = []
        for (c0, c1) in groups:
            b = Buf(name)

            def fn(e, c0=c0, c1=c1):
                return e.dma_start(out=w[:, :, c0:c1], in_=src.rearrange("(k p) f -> p k f", p=128)[:, :, c0:c1])
            P.op("pool", fn, writes=[b], dma=True)
            bufs.append(b)
        return w, bufs

    def load_w_kgroups(es, src, K_, kgroups, name):
        F_ = src.shape[1]
        nk = K_ // 128
        w, _ = kb.sb(es, [128, nk, F_], BF16, name)
        bufs = {}
        for (k0, k1) in kgroups:
            b = Buf(name)

            def fn(e, k0=k0, k1=k1):
                return e.dma_start(out=w[:, k0:k1, :], in_=src.rearrange("(k p) f -> p k f", p=128)[:, k0:k1, :])
            P.op("pool", fn, writes=[b], dma=True)
            for k in range(k0, k1):
                bufs[k] = b
        return w, bufs

    def phase_mod(L):
        with ExitStack() as es:
            cT, b_cT = kb.sb(es, [128, 2, 8], F32, "cT")
            kb.dma(cT, cc.rearrange("r (k p) -> p r k", p=128), [], [b_cT], allow_slow_non_contiguous=True)
            kb.act(cT, cT, AF.Silu, [b_cT], [b_cT])
            cl, b_cl = kb.sb(es, [128, 2, 8, 128], BF16, "cl")
            for r_ in range(2):
                kb.copy(cl[:, r_, :, :], cT[:, r_, :].unsqueeze(2).to_broadcast([128, 8, 128]), [b_cT], [b_cl])
            mb, b_mb = kb.sb(es, [128, 6 * D], F32, "mb")
            kb.bcast_load(mb, b_mb, mod_b[L, :])
            ng, b_ng = kb.sb(es, [128, 2, D], F32, "ng")
            kb.bcast_load(ng[:, 0, :], b_ng, norm1_g[L, :])
            kb.bcast_load(ng[:, 1, :], b_ng, norm2_g[L, :])
            wr = kb.sbring(es, [128, 8, 512], BF16, 4, "mw")
            pr = kb.psring(es, [128, 512], F32, 4, "mp")
            rr = kb.sbring(es, [128, 512], F32, 4, "mr")
            st = {}

            def A(gi):
                w, b_w = wr.next()
                kb.load_w(w, b_w, mod_w[L], D, gi * 512, (gi + 1) * 512)
                st[gi] = (w, b_w)

            def B(gi):
                w, b_w = st.pop(gi)
                piece, half = gi // 2, gi % 2
                for kind in range(2):
                    p_, b_p = pr.next()
                    for k in range(8):
                        kb.mm(p_, cl[:, kind, k, :], w[:, k, :], [b_cl, b_w], [b_p], start=(k == 0), stop=(k == 7))
                    r_, b_r = rr.next()
                    kb.tt(r_, p_, mb[:, gi * 512:(gi + 1) * 512], ALU.add, [b_p, b_mb], [b_r])
                    if piece in (1, 4):
                        kb.stt(r_, r_, 1.0, ng[:, 0 if piece == 1 else 1, half * 512:(half + 1) * 512], ALU.add, ALU.mult, [b_r, b_ng], [b_r])
                    idx = {0: 1, 1: 0, 2: 2, 3: 4, 4: 3, 5: 5}[piece]
                    kb.dma(MODR[kind, idx, half * 512:(half + 1) * 512], r_[0:1, :], [b_r], [b_MODR[0]])
            pipeline(list(range(12)), [A, lambda g: None, B])
        P.barrier()

    def outproj_T(rs, yb16, b_y):
        yT, b_yT = rs["yT"].next()
        for j in range(0, 16, 8):
            tp, b_tp = rs["tp"].next()
            for q in range(8):
                kb.tr(tp[:, q, :], yb16[:, (j + q) * 128:(j + q + 1) * 128], [b_y], [b_tp])
            kb.copy(yT[:, j:j + 8, :], tp, [b_tp], [b_yT], eng="act" if j else "dve")
        return yT, b_yT

    def outproj_mm(rs, c, yT, b_yT, wout, wb, G, b_G):
        t, b_t = rs["ot"].next()
        for half in range(2):
            po, b_po = rs["po"].next()
            for f in range(16):
                kb.mm(po, yT[:, f, :], wout[:, f, half * 512:(half + 1) * 512], [b_yT, wb[f]], [b_po], start=(f == 0), stop=(f == 15))
            kb.tt(t[:, half * 512:(half + 1) * 512], po, G[:, half * 512:(half + 1) * 512], ALU.mult, [b_po, b_G], [b_t])
        kb.dma(X[c * 128:(c + 1) * 128, :], t, [b_t], [b_X[c]], eng="pool", accum_op=ALU.add)

    def outproj_residual(rs, c, yb16, b_y, wout, wb, G, b_G):
        yT, b_yT = outproj_T(rs, yb16, b_y)
        outproj_mm(rs, c, yT, b_yT, wout, wb, G, b_G)

    def outproj_setup(es, wsrc):
        rs = {}
        wout, wb = load_w_kgroups(es, wsrc, 2048, [(0, 4), (4, 8), (8, 12), (12, 16)], "wout")
        rs["wout"] = (wout, wb)
        rs["yT"] = kb.sbring(es, [128, 16, 128], BF16, 2, "yT")
        rs["ot"] = kb.sbring(es, [128, D], F32, 2, "ot")
        return rs

    def phase_ffn(L):
        with ExitStack() as es:
            groups = [(i * 512, (i + 1) * 512) for i in range(5)] + [(2560, 2816)]
            w1, w1b = load_w_groups(es, ffn_w1[L], D, groups[:1], "w1")
            w3, w3b = load_w_groups(es, ffn_w3[L], D, groups[:1], "w3")
            for (c0, c1) in groups[1:]:
                for (w_, wb_, src) in ((w1, w1b, ffn_w1[L]), (w3, w3b, ffn_w3[L])):
                    b = Buf("wg")

                    def fn(e, c0=c0, c1=c1, w_=w_, src=src):
                        return e.dma_start(out=w_[:, :, c0:c1], in_=src.rearrange("(k p) f -> p k f", p=128)[:, :, c0:c1])
                    P.op("pool", fn, writes=[b], dma=True)
                    wb_.append(b)
            w2, w2b = load_w_kgroups(es, ffn_w2[L], DFF, [(0, 6), (6, 12), (12, 17), (17, 22)], "w2")
            mcA = ModC(es, [3, 4])
            mcB = ModC(es, [5])
            nr = norm_setup(es)
            xr = kb.sbring(es, [128, D], F32, 2, "fx")
            ar = kb.sbring(es, [128, D], BF16, 2, "fa")
            aTr = kb.sbring(es, [128, 8, 128], BF16, 2, "faT")
            tpr = kb.psring(es, [128, 8, 128], BF16, 2, "ftp")
            hr = kb.psring(es, [128, 512], F32, 4, "fh")
            por = kb.psring(es, [128, 512], F32, 2, "fpo")
            sr = kb.sbring(es, [128, 512], F32, 2, "fs")
            gr = kb.sbring(es, [128, DFF], BF16, 2, "fg")
            gTr = kb.sbring(es, [128, 22, 128], BF16, 2, "fgT")
            otr = kb.sbring(es, [128, D], F32, 2, "fot")
            st = {}

            def L_(c):
                xc, b_xc = xr.next()
                kb.dma(xc, X[c * 128:(c + 1) * 128, :], [b_X[c]], [b_xc])
                st[("x", c)] = (xc, b_xc)

            def N_(c):
                kd = kind_of(c)
                xc, b_xc = st.pop(("x", c))
                st[("a", c)] = norm_a(nr, xc, b_xc, *mcA.get(kd, 3), *mcA.get(kd, 4), ar)

            def A(c):
                a, b_a = st.pop(("a", c))
                aT, b_aT = aTr.next()
                trans_a(a, b_a, tpr, aT, b_aT)
                st[c] = (None, None, aT, b_aT)

            def B1(c):
                xc, b_xc, aT, b_aT = st.pop(c)
                g, b_g = gr.next()
                for gi, (f0, f1) in enumerate(groups):
                    fw = f1 - f0
                    h1, b_h1 = hr.next()
                    h3, b_h3 = hr.next()
                    for k in range(8):
                        kb.mm(h1[:, :fw], aT[:, k, :], w1[:, k, f0:f1], [b_aT, w1b[gi]], [b_h1], start=(k == 0), stop=(k == 7))
                    for k in range(8):
                        kb.mm(h3[:, :fw], aT[:, k, :], w3[:, k, f0:f1], [b_aT, w3b[gi]], [b_h3], start=(k == 0), stop=(k == 7))
                    s, b_s = sr.next()
                    kb.act(s[:, :fw], h1[:, :fw], AF.Silu, [b_h1], [b_s])
                    kb.tt(g[:, f0:f1], s[:, :fw], h3[:, :fw], ALU.mult, [b_s, b_h3], [b_g])
                st[("g", c)] = (g, b_g)

            def B2(c):
                g, b_g = st.pop(("g", c))
                gT, b_gT = gTr.next()
                for j in range(0, 22, 8):
                    nb = min(8, 22 - j)
                    tp, b_tp = tpr.next()
                    for q in range(nb):
                        kb.tr(tp[:, q, :], g[:, (j + q) * 128:(j + q + 1) * 128], [b_g], [b_tp])
                    kb.copy(gT[:, j:j + nb, :], tp[:, :nb, :], [b_tp], [b_gT], eng="act" if j == 8 else "dve")
                st[("gT", c)] = (gT, b_gT)

            def B3(c):
                kd = kind_of(c)
                gT, b_gT = st.pop(("gT", c))
                G, b_G = mcB.get(kd, 5)
                t, b_t = otr.next()
                for half in range(2):
                    po, b_po = por.next()
                    for f in range(22):
                        kb.mm(po, gT[:, f, :], w2[:, f, half * 512:(half + 1) * 512], [b_gT, w2b[f]], [b_po], start=(f == 0), stop=(f == 21))
                    kb.tt(t[:, half * 512:(half + 1) * 512], po, G[:, half * 512:(half + 1) * 512], ALU.mult, [b_po, b_G], [b_t])
                kb.dma(X[c * 128:(c + 1) * 128, :], t, [b_t], [b_X[c]], eng="pool", accum_op=ALU.add)
            pipeline(list(range(2 if L == 3 else 0, NCH)), [L_, N_, A, B1, B2, B3])
        P.barrier()

    def phase_gmlp(L):
        with ExitStack() as es:
            win, winb = load_w_groups(es, gmlp_w_in[0], D, [(i * 512, (i + 1) * 512) for i in range(8)], "gwin")
            rs = outproj_setup(es, gmlp_w_out[0])
            wout, wb = rs["wout"]
            mcA = ModC(es, [0, 1])
            mcB = ModC(es, [2])
            nr = norm_setup(es, 2048)
            ngb, b_ngb = kb.sb(es, [128, 2048], F32, "gng")
            kb.bcast_load(ngb, b_ngb, gmlp_norm_g[0, :])
            bs, b_bs = kb.sb(es, [128, 8], F32, "gbs")
            kb.dma(bs, gmlp_b_s[0].rearrange("g i -> i g"), [], [b_bs], allow_slow_non_contiguous=True)
            wsT, b_wsT = kb.sb(es, [128, 8, 128], BF16, "wsT")
            tpr = kb.psring(es, [128, 8, 128], BF16, 2, "gtp")
            rs["tp"] = tpr
            hr = kb.psring(es, [128, 512], F32, 4, "gh")
            rs["po"] = hr
            pvr = kb.psring(es, [128, 512], F32, 2, "gpv")
            with ExitStack() as es2:
                wsf, b_wsf = kb.sb(es2, [128, 8, 128], F32, "wsf")
                kb.dma(wsf, gmlp_w_s[0].rearrange("g i j -> i g j"), [], [b_wsf])
                wsb, b_wsb = kb.sb(es2, [128, 8, 128], BF16, "wsb")
                kb.copy(wsb, wsf, [b_wsf], [b_wsb])
                tp, b_tp = tpr.next()
                for g_ in range(8):
                    kb.tr(tp[:, g_, :], wsb[:, g_, :], [b_wsb], [b_tp])
                kb.copy(wsT, tp, [b_tp], [b_wsT])
                P.barrier()
            xr = kb.sbring(es, [128, D], F32, 3, "gx")
            ar = kb.sbring(es, [128, D], BF16, 2, "ga")
            aTr = kb.sbring(es, [128, 8, 128], BF16, 2, "gaT")
            ur = kb.sbring(es, [128, 2048], BF16, 2, "gu")
            vr = kb.sbring(es, [128, 2048], F32, 1, "gv")
            vnr = kb.sbring(es, [128, 2048], BF16, 2, "gvn")
            gtr = kb.sbring(es, [128, 2048], BF16, 2, "ggt")
            st = {}

            def L_(c):
                xc, b_xc = xr.next()
                kb.dma(xc, X[c * 128:(c + 1) * 128, :], [b_X[c]], [b_xc])
                st[("x", c)] = (xc, b_xc)

            def N_(c):
                kd = kind_of(c)
                xc, b_xc = st.pop(("x", c))
                st[("a", c)] = norm_a(nr, xc, b_xc, *mcA.get(kd, 0), *mcA.get(kd, 1), ar)

            def A(c):
                a, b_a = st.pop(("a", c))
                aT, b_aT = aTr.next()
                trans_a(a, b_a, tpr, aT, b_aT)
                st[c] = (None, None, aT, b_aT)

            def B1(c):
                xc, b_xc, aT, b_aT = st.pop(c)
                u, b_u = ur.next()
                v, b_v = vr.next()
                for gi in [4, 5, 6, 7, 0, 1, 2, 3]:
                    h, b_h = hr.next()
                    for k in range(8):
                        kb.mm(h, aT[:, k, :], win[:, k, gi * 512:(gi + 1) * 512], [b_aT, winb[gi]], [b_h], start=(k == 0), stop=(k == 7))
                    if gi < 4:
                        kb.act(u[:, gi * 512:(gi + 1) * 512], h, AF.Gelu_apprx_tanh, [b_h], [b_u])
                    else:
                        kb.act(v[:, (gi - 4) * 512:(gi - 3) * 512], h, AF.Gelu_apprx_tanh, [b_h], [b_v])
                ss, b_ss = rstd_of(nr, v, b_v, 2048)
                vn, b_vn = vnr.next()
                kb.stt(vn, v, ss, ngb, ALU.mult, ALU.mult, [b_v, b_ss, b_ngb], [b_vn])
                st[("u", c)] = (u, b_u, vn, b_vn)

            def B2(c):
                u, b_u, vn, b_vn = st.pop(("u", c))
                gt, b_gt = gtr.next()
                for q in range(4):
                    pv, b_pv = pvr.next()
                    for hh in range(2):
                        g_ = 2 * q + hh
                        kb.mm(pv[:, hh * 256:(hh + 1) * 256], wsT[:, g_, :], vn[:, g_ * 256:(g_ + 1) * 256], [b_wsT, b_vn], [b_pv])
                    for hh in range(2):
                        g_ = 2 * q + hh
                        kb.stt(gt[:, g_ * 256:(g_ + 1) * 256], pv[:, hh * 256:(hh + 1) * 256], bs[:, g_:g_ + 1], u[:, g_ * 256:(g_ + 1) * 256],
                               ALU.add, ALU.mult, [b_pv, b_bs, b_u], [b_gt])
                st[("gt", c)] = (gt, b_gt)

            def B3(c):
                gt, b_gt = st.pop(("gt", c))
                st[("yT", c)] = outproj_T(rs, gt, b_gt)

            def B4(c):
                kd = kind_of(c)
                yT, b_yT = st.pop(("yT", c))
                outproj_mm(rs, c, yT, b_yT, wout, wb, *mcB.get(kd, 2))
            pipeline(list(range(NCH)), [L_, N_, A, B1, B2, B3, B4])
        P.barrier()

    def phase_finish(kindname, j, wsrc, skip_ctx=False):
        ssd = kindname == "ssd"
        with ExitStack() as es:
            rs = outproj_setup(es, wsrc)
            wout, wb = rs["wout"]
            rs["tp"] = kb.psring(es, [128, 8, 128], BF16, 2, "ntp")
            rs["po"] = kb.psring(es, [128, 512], F32, 2, "npo")
            psr = kb.psring(es, [128, 512], F32, 4, "nps")
            mc = ModC(es, [2])
            if ssd:
                nr = norm_setup(es, 2048, tmp=False)
                ngb, b_ngb = kb.sb(es, [128, 2048], F32, "sng")
                kb.bcast_load(ngb, b_ngb, ssd_norm_g[j, :])
                Db, b_Db = kb.sb(es, [128, 32], F32, "Db")
                kb.bcast_load(Db, b_Db, ssd_d[j, :])
                xsr = kb.sbring(es, [128, 32, 64], BF16, 3, "nxs")
                xdr = kb.sbring(es, [128, 2048], BF16, 2, "nxd")
            else:
                nr = {"ss4": kb.sbring(es, [128, 4], F32, 3, "ss4"), "junk": kb.sb(es, [128, 512], BF16, "wjunk")}
            yfr = kb.sbring(es, [128, 2048], BF16, 3, "nyf")
            ybr = kb.sbring(es, [128, 2048], BF16, 3, "nyb")
            gzr = kb.sbring(es, [128, 2048], BF16, 3, "ngz")
            y2r = kb.sbring(es, [128, 2048], F32, 3, "ny2")
            y3r = kb.sbring(es, [128, 2048], BF16, 4, "ny3")
            rs["yT"] = kb.sbring(es, [128, 16, 128], BF16, 3, "yT3")
            if not ssd:
                nr["ss4"] = kb.sbring(es, [128, 4], F32, 4, "ss4b")
            st = {}

            def A(c):
                rows = slice(c * 128, (c + 1) * 128)
                yf, b_yf = yfr.next()
                kb.dma(yf, YF[rows, :], [b_YF[c]], [b_yf])
                yb_, b_yb = ybr.next()
                kb.dma(yb_, YB[rows, :], [b_YB[c]], [b_yb])
                gz, b_gz = gzr.next()
                kb.dma(gz, (ZS if ssd else GS)[rows, :], [(b_ZS if ssd else b_GS)[c]], [b_gz])
                xs = b_xs = None
                if ssd:
                    xs, b_xs = xsr.next()
                    kb.dma(xs, XS[rows, :].rearrange("p (h d) -> p h d", d=64), [b_XS[c]], [b_xs])
                st[c] = (yf, b_yf, yb_, b_yb, gz, b_gz, xs, b_xs)

            def Ba(c):
                yf, b_yf, yb_, b_yb, gz, b_gz, xs, b_xs = st.pop(c)
                if ssd:
                    y2, b_y2 = y2r.next()
                    xd, b_xd = xdr.next()
                    kb.tt(xd.rearrange("p (h d) -> p h d", d=64), xs, Db.unsqueeze(2).to_broadcast([128, 32, 64]), ALU.mult, [b_xs, b_Db], [b_xd])
                    for q in range(4):
                        cs_ = slice(q * 512, (q + 1) * 512)
                        ps, b_ps = psr.next()
                        kb.mm(ps, kb.ident, yf[:, cs_], [kb.b_ident, b_yf], [b_ps], start=True, stop=False)
                        kb.mm(ps, kb.ident, yb_[:, cs_], [kb.b_ident, b_yb], [b_ps], start=False, stop=False)
                        kb.mm(ps, kb.ident, xd[:, cs_], [kb.b_ident, b_xd], [b_ps], start=False, stop=True)
                        kb.tt(y2[:, cs_], ps, gz[:, cs_], ALU.mult, [b_ps, b_gz], [b_y2])
                    st[("y2", c)] = (y2, b_y2, None, None, None, None)
                else:
                    ss, b_ss = nr["ss4"].next()
                    junk, b_junk = nr["junk"]
                    y3, b_y3 = y3r.next()
                    pss = []
                    for q in range(4):
                        cs_ = slice(q * 512, (q + 1) * 512)
                        ps, b_ps = psr.next()
                        kb.mm(ps, kb.ident, yf[:, cs_], [kb.b_ident, b_yf], [b_ps], start=True, stop=False)
                        kb.mm(ps, kb.ident, yb_[:, cs_], [kb.b_ident, b_yb], [b_ps], start=False, stop=True)
                        kb.act(junk, ps, AF.Square, [b_ps], [b_junk, b_ss], accum_out=ss[:, q:q + 1])
                        pss.append((ps, b_ps))
                    kb.ts(ss, ss, 1.0 / 512, EPS, ALU.mult, ALU.add, [b_ss], [b_ss])
                    kb.tt(ss, ss, negh[:, 0:4], ALU.pow, [b_ss, b_negh], [b_ss], eng="pool")
                    for q, (ps, b_ps) in enumerate(pss):
                        cs_ = slice(q * 512, (q + 1) * 512)
                        kb.stt(y3[:, cs_], ps, ss[:, q:q + 1], gz[:, cs_], ALU.mult, ALU.mult, [b_ps, b_ss, b_gz], [b_y3])
                    st[("y2", c)] = (y3, b_y3, None, None, None, None)

            def Bb(c):
                y2, b_y2, ss, b_ss, gz, b_gz = st.pop(("y2", c))
                if ssd:
                    y3, b_y3 = y3r.next()
                    ss, b_ss = rstd_of(nr, y2, b_y2, 2048)
                    kb.stt(y3, y2, ss, ngb, ALU.mult, ALU.mult, [b_y2, b_ss, b_ngb], [b_y3])
                else:
                    y3, b_y3 = y2, b_y2
                st[("y", c)] = (y3, b_y3)

            def B2a(c):
                y3, b_y3 = st.pop(("y", c))
                st[("yT", c)] = outproj_T(rs, y3, b_y3)

            def B2b(c):
                kd = kind_of(c)
                yT, b_yT = st.pop(("yT", c))
                outproj_mm(rs, c, yT, b_yT, wout, wb, *mc.get(kd, 2))
            pipeline(list(range(2 if skip_ctx else 0, NCH)), [A, Ba, Bb, B2a, B2b])
        P.barrier()

    def phase_ssd(L, j):
        with ExitStack() as es:
            wg = [(i * 512, (i + 1) * 512) for i in range(10)] + [(5120, 5184)]
            order = [0, 1, 2, 3, 10, 4, 5, 6, 7, 8, 9]
            win, winb_l = load_w_groups(es, ssd_w_in[j], D, [wg[i] for i in order], "swin")
            winb = {order[i]: winb_l[i] for i in range(len(order))}
            mc = ModC(es, [0, 1])
            nr = norm_setup(es)
            dtb, b_dtb = kb.sb(es, [128, 64], F32, "dtb")
            kb.bcast_load(dtb, b_dtb, ssd_dtb[j, :])
            Ab, b_Ab = kb.sb(es, [128, 64], F32, "Ab")
            kb.bcast_load(Ab, b_Ab, ssd_alog[j, :])
            kb.act(Ab, Ab, AF.Exp, [b_Ab], [b_Ab])
            kb.ts(Ab, Ab, -1.0, None, ALU.mult, None, [b_Ab], [b_Ab])
            zt, b_zt = kb.sb(es, [128, 8], BF16, "zt")
            kb.memset(zt, 0.0, [b_zt])
            for fc in range(24):
                rowsl = slice(fc * 128, (fc + 1) * 128)
                kb.dma(RAWT[rowsl, 0:4], zt[:, 0:4], [b_zt], [b_RAWT[fc]])
                kb.dma(RAWT[rowsl, 4 + CTX:4 + CTX + 8], zt, [b_zt], [b_RAWT[fc]])
                kb.dma(RAWT[rowsl, RAWW - 4:RAWW], zt[:, 0:4], [b_zt], [b_RAWT[fc]])
            xr = kb.sbring(es, [128, D], F32, 3, "sx")
            ar = kb.sbring(es, [128, D], BF16, 8, "sa")
            aTt = kb.sbring(es, [128, 8, 512], BF16, 2, "saT")
            tpr = kb.psring(es, [128, 8, 128], BF16, 2, "stp")
            hr = kb.psring(es, [128, 512], F32, 3, "sh")
            pdt = kb.psring(es, [128, 4, 64], F32, 1, "spd")
            zr = kb.sbring(es, [128, 2048], BF16, 2, "sz")
            rawr = kb.sbring(es, [128, 512], BF16, 4, "sraw")
            dr = kb.sbring(es, [128, 5, 4, 64], F32, 2, "sd")
            tiles = [(0, 2)] + [(2 + 4 * t, 4) for t in range(8)]
            st = {}
            cw, b_cw = kb.sb(es, [128, 24, 5], F32, "cw")
            for k in range(5):
                kb.dma(cw[:, :, k], ssd_conv_w[j, k, :].rearrange("(f p) -> p f", p=128), [], [b_cw], allow_slow_non_contiguous=True)
            cb, b_cb = kb.sb(es, [128, 24], F32, "cb")
            kb.dma(cb, ssd_conv_b[j, :].rearrange("(f p) -> p f", p=128), [], [b_cb], allow_slow_non_contiguous=True)
            NW = 3
            rwr = kb.sbring(es, [128, 516], BF16, 3 * NW, "crw")
            accr = kb.sbring(es, [128, 512], F32, 2 * NW, "cacc")
            actr = kb.sbring(es, [128, 512], BF16, 3 * NW, "cact")
            ctpr = kb.psring(es, [128, 4, 128], BF16, 2, "ctp")
            str_ = kb.sbring(es, [128, 4, 128], BF16, 2 * NW, "cst")

            def N_(ti):
                c0, ncn = tiles[ti]
                l = []
                for q in range(ncn):
                    c = c0 + q
                    kd = kind_of(c)
                    xc, b_xc = xr.next()
                    kb.dma(xc, X[c * 128:(c + 1) * 128, :], [b_X[c]], [b_xc])
                    l.append(norm_a(nr, xc, b_xc, *mc.get(kd, 0), *mc.get(kd, 1), ar))
                st[("a", ti)] = l

            def A(ti):
                c0, ncn = tiles[ti]
                aT, b_aT = aTt.next()
                for q, (a, b_a) in enumerate(st.pop(("a", ti))):
                    trans_a(a, b_a, tpr, aT[:, :, q * 128:(q + 1) * 128], b_aT)
                st[ti] = (aT, b_aT)

            def B(ti):
                c0, ncn = tiles[ti]
                aT, b_aT = st.pop(ti)
                TT = ncn * 128
                seg = 0 if ti == 0 else 1
                col0 = COL[seg] + (0 if ti == 0 else (ti - 1) * 512)
                for q in range(ncn):
                    c = c0 + q
                    z, b_z = zr.next()
                    for gi in range(4):
                        h, b_h = hr.next()
                        for k in range(8):
                            kb.mm(h, aT[:, k, q * 128:(q + 1) * 128], win[:, k, gi * 512:(gi + 1) * 512], [b_aT, winb[gi]], [b_h], start=(k == 0), stop=(k == 7))
                        kb.act(z[:, gi * 512:(gi + 1) * 512], h, AF.Silu, [b_h], [b_z])
                    kb.dma(ZS[c * 128:(c + 1) * 128, :], z, [b_z], [b_ZS[c]])
                    yield
                pd, b_pd = pdt.next()
                for q in range(ncn):
                    for k in range(8):
                        kb.mm(pd[:, q, :], aT[:, k, q * 128:(q + 1) * 128], win[:, k, 5120:5184], [b_aT, winb[10]], [b_pd], start=(k == 0), stop=(k == 7))
                d_, b_d = dr.next()
                n_ = ncn
                kb.tt(d_[:, 0, :n_, :], pd[:, :n_, :], dtb.unsqueeze(1).to_broadcast([128, n_, 64]), ALU.add, [b_pd, b_dtb], [b_d])
                kb.act(d_[:, 1, :n_, :], d_[:, 0, :n_, :], AF.Abs, [b_d], [b_d])
                kb.act(d_[:, 1, :n_, :], d_[:, 1, :n_, :], AF.Exp, [b_d], [b_d], scale=-1.0)
                kb.act(d_[:, 1, :n_, :], d_[:, 1, :n_, :], AF.Ln, [b_d], [b_d], bias=1.0)
                kb.stt(d_[:, 2, :n_, :], d_[:, 0, :n_, :], 0.0, d_[:, 1, :n_, :], ALU.max, ALU.add, [b_d], [b_d])
                kb.tt(d_[:, 3, :n_, :], d_[:, 2, :n_, :], Ab.unsqueeze(1).to_broadcast([128, n_, 64]), ALU.mult, [b_d, b_Ab], [b_d])
                for q in range(ncn):
                    c = c0 + q
                    kb.dma(DT[c * 128:(c + 1) * 128, :], d_[:, 2, q, :], [b_d], [b_DT[c]])
                    kb.dma(LA[c * 128:(c + 1) * 128, :], d_[:, 3, q, :], [b_d], [b_LA[c]])
                yield
                for fc in range(24):
                    h, b_h = hr.next()
                    gi = 4 + fc // 4
                    for k in range(8):
                        kb.mm(h[:, :TT], win[:, k, 2048 + fc * 128:2048 + (fc + 1) * 128], aT[:, k, :TT], [b_aT, winb[gi]], [b_h], start=(k == 0), stop=(k == 7))
                    rw, b_rw = rawr.next()
                    kb.copy(rw[:, :TT], h[:, :TT], [b_h], [b_rw], eng="act")
                    kb.dma(RAWT[fc * 128:(fc + 1) * 128, col0:col0 + TT], rw[:, :TT], [b_rw], [b_RAWT[fc]])
                    yield
            def CV(ti):
                c0, ncn = tiles[ti]
                TT = ncn * 128
                seg = 0 if ti == 0 else 1
                off = 0 if ti == 0 else (ti - 1) * 512
                col0 = COL[seg] + off
                t0 = (0 if seg == 0 else CTX) + off
                nb = TT // 128
                items = [list(range(f, f + NW)) for f in range(0, 24, NW)]
                st2 = {}

                def A2(fcs):
                    l = []
                    for fc in fcs:
                        rw, b_rw = rwr.next()
                        kb.dma(rw[:, :TT + 4], RAWT[fc * 128:(fc + 1) * 128, col0 - 2:col0 + TT + 2], [b_RAWT[fc]], [b_rw])
                        l.append((rw, b_rw))
                    st2[("A", fcs[0])] = l

                def B2(fcs):
                    l = st2.pop(("A", fcs[0]))
                    accs = [accr.next() for _ in fcs]
                    for k in range(5):
                        for fc, (rw, b_rw), (acc, b_acc) in zip(fcs, l, accs):
                            if k == 0:
                                kb.ts(acc[:, :TT], rw[:, 0:TT], cw[:, fc, 0:1], cb[:, fc:fc + 1], ALU.mult, ALU.add, [b_rw, b_cw, b_cb], [b_acc])
                            else:
                                kb.stt(acc[:, :TT], rw[:, k:k + TT], cw[:, fc, k:k + 1], acc[:, :TT], ALU.mult, ALU.add, [b_rw, b_cw, b_acc], [b_acc])
                    l2 = []
                    for fc, (acc, b_acc) in zip(fcs, accs):
                        a_, b_a = actr.next()
                        kb.act(a_[:, :TT], acc[:, :TT], AF.Silu, [b_acc], [b_a])
                        l2.append((a_, b_a))
                    st2[("B", fcs[0])] = l2

                def C2(fcs):
                    l2 = st2.pop(("B", fcs[0]))
                    for fc, (a_, b_a) in zip(fcs, l2):
                        if fc < 20:
                            tp, b_tp = ctpr.next()
                            for q in range(nb):
                                kb.tr(tp[:, q, :], a_[:, q * 128:(q + 1) * 128], [b_a], [b_tp])
                            s_, b_st = str_.next()
                            kb.copy(s_[:, :nb, :], tp[:, :nb, :], [b_tp], [b_st], eng="act")
                            if fc < 16:
                                dst = XS[t0:t0 + TT, fc * 128:(fc + 1) * 128]
                                bd = [b_XS[t0 // 128 + q] for q in range(nb)]
                            else:
                                dst = KB_[t0:t0 + TT, (fc - 16) * 128:(fc - 15) * 128]
                                bd = [b_KB[t0 // 128 + q] for q in range(nb)]
                            kb.dma(dst.rearrange("(b p) f -> p b f", p=128), s_[:, :nb, :], [b_st], bd)
                        if 16 <= fc < 20:
                            kb.dma(BT[fc - 16, :, t0:t0 + TT], a_[:, :TT], [b_a], [b_BT[t0 // 128 + q] for q in range(nb)])
                        if fc >= 20:
                            kb.dma(CT[fc - 20, :, t0:t0 + TT], a_[:, :TT], [b_a], [b_CT[t0 // 128 + q] for q in range(nb)])
                stages2 = [A2, B2, lambda f: None, C2]
                n2, S2 = len(items), len(stages2)
                for step in range(n2 + S2 - 1):
                    for si in range(S2):
                        i = step - si
                        if 0 <= i < n2:
                            stages2[si](items[i])
                    yield
            nt = len(tiles)
            for s_ in range(nt + 4):
                if s_ < nt:
                    N_(s_)
                if 0 <= s_ - 1 < nt:
                    A(s_ - 1)
                gb = B(s_ - 2) if 0 <= s_ - 2 < nt else None
                gc = CV(s_ - 4) if 0 <= s_ - 4 < nt else None
                while gb is not None or gc is not None:
                    if gc is not None:
                        try:
                            next(gc)
                        except StopIteration:
                            gc = None
                    for _ in range(3):
                        if gb is None:
                            break
                        try:
                            next(gb)
                        except StopIteration:
                            gb = None
        P.barrier()
        with ExitStack() as es:
            pseg = kb.psring(es, [128, 4, 128], F32, 3, "ppseg")
            pyr = kb.psring(es, [128, 8, 64], F32, 1, "ppy")
            pir = kb.psring(es, [128, 8, 64], F32, 1, "ppi")
            pstr = kb.psring(es, [128, 512], F32, 1, "ppst")

            def make_sweep(sweep):
                fin = sweep == "f"
                U = msk["U" + sweep]
                SL = msk["SL" + sweep]
                M4 = msk["M" + sweep]
                hs = 0 if sweep == "f" else 32
                YD, b_YD = (YF, b_YF) if fin else (YB, b_YB)
                S, b_S = kb.sb(es, [128, 32, 64], F32, "S")
                Sb, b_Sb = kb.sb(es, [128, 2048], BF16, "Sb")
                kb.memset(S, 0.0, [b_S])
                kb.memset(Sb, 0.0, [b_Sb])
                bankA, _ = kb.ps(es, [128, 512], F32, "bankA")
                psc = bankA[:, 0:256].rearrange("p (g t) -> p g t", t=128)
                b_psc = Buf("psc")
                pcv = bankA[:, 256:352]
                b_pcv = Buf("pc")
                xsr = kb.sbring(es, [128, 32, 64], BF16, 3, "qxs")
                kr = kb.sbring(es, [128, 512], BF16, 5, "qk")
                btr = kb.sbring(es, [128, 4, 128], BF16, 4, "qbt")
                ctr = kb.sbring(es, [128, 4, 128], BF16, 5, "qct")
                dtr = kb.sbring(es, [128, 2, 64], F32, 4, "qdt")
                la16r = kb.sbring(es, [128, 32], BF16, 2, "qla")
                r1ar = kb.sbring(es, [128, 16, 128], BF16, 2, "qr1a")
                r1br = kb.sbring(es, [128, 16, 128], BF16, 2, "qr1b")
                exr = kb.sbring(es, [128, 96], F32, 4, "qex")
                vdr = kb.sbring(es, [128, 32, 64], BF16, 2, "qvd")
                vtr = kb.sbring(es, [128, 32, 64], BF16, 2, "qvt")
                scr = kb.sbring(es, [128, 4, 128], BF16, 2, "qsc")
                Er = kb.sbring(es, [128, 4, 128], BF16, 3, "qE")
                atr = kb.sbring(es, [128, 4, 128], BF16, 3, "qat")
                t1r = kb.sbring(es, [128, 8, 64], F32, 2, "qt1")
                ydr = kb.sbring(es, [128, 2048], BF16, 2, "qyd")
                order = list(range(NCH)) if fin else [1, 0] + list(range(NCH - 1, 1, -1))
                st = {}

                def A(c):
                    rows = slice(c * 128, (c + 1) * 128)
                    xs, b_xs = xsr.next()
                    kb.dma(xs, XS[rows, :].rearrange("p (h d) -> p h d", d=64), [b_XS[c]], [b_xs])
                    kk, b_kk = kr.next()
                    kb.dma(kk, KB_[rows, 0:512], [b_KB[c]], [b_kk])
                    bt, b_bt = btr.next()
                    kb.dma(bt, BT[:, :, rows].rearrange("g n t -> n g t"), [b_BT[c]], [b_bt])
                    ct, b_ct = ctr.next()
                    kb.dma(ct, CT[:, :, rows].rearrange("g n t -> n g t"), [b_CT[c]], [b_ct])
                    dl, b_dl = dtr.next()
                    kb.dma(dl[:, 0, :], DT[rows, :], [b_DT[c]], [b_dl])
                    kb.dma(dl[:, 1, :], LA[rows, :], [b_LA[c]], [b_dl])
                    st[c] = dict(xs=xs, b_xs=b_xs, kk=kk, b_kk=b_kk, bt=bt, b_bt=b_bt, ct=ct, b_ct=b_ct, dl=dl, b_dl=b_dl)
                    yield

                def B(c):
                    s_ = st[c]
                    dl, b_dl, xs, b_xs, bt, b_bt, ct, b_ct = s_["dl"], s_["b_dl"], s_["xs"], s_["b_xs"], s_["bt"], s_["b_bt"], s_["ct"], s_["b_ct"]
                    dtv = dl[:, 0, hs:hs + 32]
                    la = dl[:, 1, hs:hs + 32]
                    la16, b_la16 = la16r.next()
                    kb.copy(la16, la, [b_dl], [b_la16], eng="act")
                    kb.mm(pcv[:, 0:32], U[2], la16, [U[3], b_la16], [b_pcv])
                    kb.mm(pcv[:, 32:64], SL[2], la16, [SL[3], b_la16], [b_pcv])
                    kb.mm(pcv[:, 64:96], ones, la16, [b_ones, b_la16], [b_pcv])
                    ex, b_ex = exr.next()
                    kb.act(ex, pcv, AF.Exp, [b_pcv], [b_ex])
                    s_.update(ex=ex, b_ex=b_ex)
                    yield

                def B2_(c):
                    s_ = st[c]
                    dl, b_dl, xs, b_xs, bt, b_bt, ct, b_ct = s_["dl"], s_["b_dl"], s_["xs"], s_["b_xs"], s_["bt"], s_["b_bt"], s_["ct"], s_["b_ct"]
                    ex, b_ex = s_["ex"], s_["b_ex"]
                    dtv = dl[:, 0, hs:hs + 32]
                    la = dl[:, 1, hs:hs + 32]
                    r1a, b_r1a = r1ar.next()
                    r1b, b_r1b = r1br.next()
                    kb.tt(r1a, U[0].unsqueeze(1).to_broadcast([128, 16, 128]), la[:, 0:16].unsqueeze(2).to_broadcast([128, 16, 128]), ALU.mult,
                          [U[1], b_dl], [b_r1a], eng="pool")
                    kb.tt(r1b, U[0].unsqueeze(1).to_broadcast([128, 16, 128]), la[:, 16:32].unsqueeze(2).to_broadcast([128, 16, 128]), ALU.mult,
                          [U[1], b_dl], [b_r1b], eng="dve")
                    s_.update(r1=(r1a, r1b), b_r1=(b_r1a, b_r1b))
                    vd, b_vd = vdr.next()
                    kb.tt(vd, xs, dtv.unsqueeze(2).to_broadcast([128, 32, 64]), ALU.mult, [b_xs, b_dl], [b_vd], eng="pool")
                    vt, b_vt = vtr.next()
                    kb.tt(vt, vd, ex[:, 32:64].unsqueeze(2).to_broadcast([128, 32, 64]), ALU.mult, [b_vd, b_ex], [b_vt], eng="pool")
                    sc, b_sc = scr.next()
                    for half in range(2):
                        for gg in range(2):
                            g_ = 2 * half + gg
                            kb.mm(psc[:, gg, :], bt[:, g_, :], ct[:, g_, :], [b_bt, b_ct], [b_psc])
                        kb.copy(sc[:, 2 * half:2 * half + 2, :], psc, [b_psc], [b_sc], eng="act")
                    s_.update(vd=vd, b_vd=b_vd, vt=vt, b_vt=b_vt, sc=sc, b_sc=b_sc)
                    yield

                def C(c):
                    s_ = st.pop(c)
                    rows = slice(c * 128, (c + 1) * 128)
                    r1, b_r1, ex, b_ex, vd, b_vd, vt, b_vt, sc, b_sc = (s_[k] for k in ("r1", "b_r1", "ex", "b_ex", "vd", "b_vd", "vt", "b_vt", "sc", "b_sc"))
                    ct, b_ct, kk, b_kk = s_["ct"], s_["b_ct"], s_["kk"], s_["b_kk"]
                    ecum, etot = ex[:, 0:32], ex[:, 64:96]
                    yd, b_yd = ydr.next()
                    segs = {}

                    def seg(q):
                        pg, b_pg = pseg.next()
                        kb.mm(pg, SL[2], r1[q // 4][:, 4 * (q % 4):4 * (q % 4) + 4, :], [SL[3], b_r1[q // 4]], [b_pg], start=True, stop=False)
                        kb.mm(pg, kb.ident, M4[2], [kb.b_ident, M4[3]], [b_pg], start=False, stop=True)
                        E, b_E = Er.next()
                        kb.act(E, pg, AF.Exp, [b_pg], [b_E])
                        at, b_at = atr.next()
                        kb.tt(at, E, sc[:, q // 2, :].unsqueeze(1).to_broadcast([128, 4, 128]), ALU.mult, [b_E, b_sc], [b_at])
                        segs[q] = (at, b_at)
                    seg(0)
                    for g_ in range(4):
                        seg(2 * g_ + 1)
                        py, b_py = pyr.next()
                        for qq in range(2):
                            q = 2 * g_ + qq
                            at, b_at = segs.pop(q)
                            for hh in range(4):
                                h = 4 * q + hh
                                kb.mm(py[:, h % 8, :], at[:, hh, :], vd[:, h, :], [b_at, b_vd], [b_py])
                            if qq == 0 and q + 2 < 8:
                                seg(q + 2)
                        pi, b_pi = pir.next()
                        kb.mm(pi, ct[:, g_, :], Sb[:, g_ * 512:(g_ + 1) * 512].rearrange("p (h d) -> p h d", d=64), [b_ct, b_Sb], [b_pi])
                        t1, b_t1 = t1r.next()
                        kb.tt(t1, pi, ecum[:, 8 * g_:8 * g_ + 8].unsqueeze(2).to_broadcast([128, 8, 64]), ALU.mult, [b_pi, b_ex], [b_t1])
                        kb.tt(yd[:, g_ * 512:(g_ + 1) * 512].rearrange("p (h d) -> p h d", d=64), py, t1, ALU.add, [b_py, b_t1], [b_yd])
                        yield
                    kb.tt(S, S, etot.unsqueeze(2).to_broadcast([128, 32, 64]), ALU.mult, [b_S, b_ex], [b_S], eng="pool")
                    for g_ in range(4):
                        pt, b_pt = pstr.next()
                        kb.mm(pt, kk[:, g_ * 128:(g_ + 1) * 128], vt[:, 8 * g_:8 * g_ + 8, :], [b_kk, b_vt], [b_pt])
                        Sg = S[:, 8 * g_:8 * g_ + 8, :]
                        kb.tt(Sg, Sg, pt.rearrange("p (h d) -> p h d", d=64), ALU.add, [b_S, b_pt], [b_S])
                    kb.copy(Sb.rearrange("p (h d) -> p h d", d=64), S, [b_S], [b_Sb], eng="act")
                    kb.dma(YD[rows, :], yd, [b_yd], [b_YD[c]])
                    yield
                stages = [A, B, B2_, C]
                n, S_ = len(order), 4
                for step in range(n + S_ - 1):
                    for si in range(S_):
                        i = step - si
                        if 0 <= i < n:
                            yield from stages[si](order[i])
            gens = [make_sweep("b"), make_sweep("f")]
            while gens:
                for g in list(gens):
                    try:
                        next(g)
                    except StopIteration:
                        gens.remove(g)
        P.barrier()
        phase_finish("ssd", j, ssd_w_out[j], skip_ctx=(L == 3))

    def phase_ret(L):
        TWO_PI = 2.0 * math.pi
        with ExitStack() as es:
            win, winb = load_w_groups(es, ret_w_in[0], D, [(i * 512, (i + 1) * 512) for i in range(12)], "rwin")
            mc = ModC(es, [0, 1])
            nr = norm_setup(es)
            cosC, b_cosC = kb.sb(es, [128, 64], F32, "cosC")
            sinC, b_sinC = kb.sb(es, [128, 64], F32, "sinC")
            cosR, b_cosR = kb.sb(es, [128, 32, 64], F32, "cosR")
            sinR, b_sinR = kb.sb(es, [128, 32, 64], F32, "sinR")
            with ExitStack() as es2:
                idxi, b_idxi = kb.sb(es2, [128, 64], I32, "idxi")
                kb.iota(idxi, [[1, 64]], 0, [b_idxi])
                inv, b_inv = kb.sb(es2, [128, 64], F32, "inv")
                kb.copy(inv, idxi, [b_idxi], [b_inv])
                kb.act(inv, inv, AF.Exp, [b_inv], [b_inv], scale=-math.log(10000.0) / 64.0)
                pi_, b_pi = kb.sb(es2, [128, 1], I32, "pi")
                kb.iota(pi_, [[0, 1]], 1, [b_pi])
                pf, b_pf = kb.sb(es2, [128, 4], F32, "pf")
                kb.copy(pf[:, 0:1], pi_, [b_pi], [b_pf])
                kb.ts(pf[:, 1:2], pf[:, 0:1], 64.0, None, ALU.is_ge, None, [b_pf], [b_pf])
                kb.stt(pf[:, 2:3], pf[:, 1:2], -64.0, pf[:, 0:1], ALU.mult, ALU.add, [b_pf], [b_pf])
                rowv, b_rowv = kb.sb(es2, [128, 32], F32, "rowv")
                ci, b_ci = kb.sb(es2, [128, 32], I32, "ci")
                kb.iota(ci, [[2, 32]], 0, [b_ci])
                kb.copy(rowv, ci, [b_ci], [b_rowv])
                kb.ts(rowv, rowv, pf[:, 1:2], None, ALU.add, None, [b_rowv, b_pf], [b_rowv])
                ang, b_ang = kb.sb(es2, [128, 33, 64], F32, "ang")
                kb.ts(ang[:, 32, :], inv, pf[:, 2:3], None, ALU.mult, None, [b_inv, b_pf], [b_ang])
                kb.tt(ang[:, 0:32, :], rowv.unsqueeze(2).to_broadcast([128, 32, 64]), inv.unsqueeze(1).to_broadcast([128, 32, 64]), ALU.mult,
                      [b_rowv, b_inv], [b_ang])
                red, b_red = kb.sb(es2, [128, 33, 64], F32, "red")
                ki, b_ki = kb.sb(es2, [128, 33, 64], I32, "ki")
                kf, b_kf = kb.sb(es2, [128, 33, 64], F32, "kf")
                for which, shift in (("sin", 0.0), ("cos", math.pi / 2)):
                    kb.ts(red, ang, shift, None, ALU.add, None, [b_ang], [b_red])
                    kb.ts(kf, red, 1.0 / TWO_PI, None, ALU.mult, None, [b_red], [b_kf])
                    kb.copy(ki, kf, [b_kf], [b_ki])
                    kb.copy(kf, ki, [b_ki], [b_kf])
                    kb.stt(red, kf, -TWO_PI, red, ALU.mult, ALU.add, [b_kf, b_red], [b_red])
                    kb.ts(kf, red, math.pi, -TWO_PI, ALU.is_gt, ALU.mult, [b_red], [b_kf])
                    kb.tt(red, red, kf, ALU.add, [b_red, b_kf], [b_red])
                    kb.ts(kf, red, -math.pi, TWO_PI, ALU.is_lt, ALU.mult, [b_red], [b_kf])
                    kb.tt(red, red, kf, ALU.add, [b_red, b_kf], [b_red])
                    kb.ts(red, red, math.pi, -math.pi, ALU.min, ALU.max, [b_red], [b_red])
                    dR, bR, dC, bC = (sinR, b_sinR, sinC, b_sinC) if which == "sin" else (cosR, b_cosR, cosC, b_cosC)
                    kb.act(dR, red[:, 0:32, :], AF.Sin, [b_red], [bR])
                    kb.act(dC, red[:, 32, :], AF.Sin, [b_red], [bC])
                P.barrier()
            csr = kb.sbring(es, [128, 2, 2, 64], F32, 2, "cs")
            for _ in range(2):
                cs_, b_cs_ = csr.next()
                kb.copy(cs_[:, 0, 1, :], cosC, [b_cosC], [b_cs_])
                kb.copy(cs_[:, 1, 1, :], sinC, [b_sinC], [b_cs_])
            xr = kb.sbring(es, [128, D], F32, 2, "rx")
            ar = kb.sbring(es, [128, D], BF16, 2, "ra")
            aTr = kb.sbring(es, [128, 8, 128], BF16, 2, "raT")
            tpr = kb.psring(es, [128, 8, 128], BF16, 2, "rtp")
            hr = kb.psring(es, [128, 512], F32, 4, "rh")
            qkr = kb.sbring(es, [128, 2, 1024], F32, 2, "rqk")
            qkb = kb.sbring(es, [128, 2, 1024], BF16, 3, "rqkb")
            tA = kb.sbring(es, [128, 8, 64], F32, 2, "rtA")
            tB = kb.sbring(es, [128, 8, 64], F32, 2, "rtB")
            tTr = kb.sbring(es, [128, 8, 128], BF16, 2, "rtT")
            vbr = kb.sbring(es, [128, 2048], BF16, 1, "rvb")
            gbr = kb.sbring(es, [128, 2048], BF16, 2, "rgb")
            st = {}

            def N_(c):
                kd = kind_of(c)
                xc, b_xc = xr.next()
                kb.dma(xc, X[c * 128:(c + 1) * 128, :], [b_X[c]], [b_xc])
                st[("a", c)] = norm_a(nr, xc, b_xc, *mc.get(kd, 0), *mc.get(kd, 1), ar)

            def A(c):
                a, b_a = st.pop(("a", c))
                aT, b_aT = aTr.next()
                trans_a(a, b_a, tpr, aT, b_aT)
                st[c] = (aT, b_aT)

            def B(c):
                rows = slice(c * 128, (c + 1) * 128)
                aT, b_aT = st.pop(c)
                qk, b_qk = qkr.next()
                vb, b_vb = vbr.next()
                gb, b_gb = gbr.next()
                for gi in range(12):
                    h, b_h = hr.next()
                    for k in range(8):
                        kb.mm(h, aT[:, k, :], win[:, k, gi * 512:(gi + 1) * 512], [b_aT, winb[gi]], [b_h], start=(k == 0), stop=(k == 7))
                    if gi < 2:
                        kb.copy(qk[:, 0, gi * 512:(gi + 1) * 512], h, [b_h], [b_qk], eng="act")
                    elif gi < 4:
                        kb.act(qk[:, 1, (gi - 2) * 512:(gi - 1) * 512], h, AF.Copy, [b_h], [b_qk], scale=1.0 / 16.0)
                    elif gi < 8:
                        kb.copy(vb[:, (gi - 4) * 512:(gi - 3) * 512], h, [b_h], [b_vb], eng="dve")
                    else:
                        kb.act(gb[:, (gi - 8) * 512:(gi - 7) * 512], h, AF.Silu, [b_h], [b_gb])
                kb.dma(VV[rows, :], vb, [b_vb], [b_VV[c]])
                kb.dma(GS[rows, :], gb, [b_gb], [b_GS[c]])
                st[("q", c)] = (qk, b_qk)

            def C(c):
                rows = slice(c * 128, (c + 1) * 128)
                qk, b_qk = st.pop(("q", c))
                qb, b_qb = qkb.next()
                if c < 2:
                    kb.copy(qb, qk, [b_qk], [b_qb], eng="pool")
                else:
                    cs_, b_cs_ = csr.next()
                    kb.copy(cs_[:, 0, 0, :], cosR[:, c - 2, :], [b_cosR], [b_cs_], eng="pool")
                    kb.copy(cs_[:, 1, 0, :], sinR[:, c - 2, :], [b_sinR], [b_cs_], eng="pool")
                    for w_ in range(2):
                        v6 = qk[:, w_, :].rearrange("p (a h d) -> p a h d", h=2, d=64)
                        o6 = qb[:, w_, :].rearrange("p (a h d) -> p a h d", h=2, d=64)
                        t1_, t2_ = v6[:, :, 0, :], v6[:, :, 1, :]
                        csb = cs_[:, 0, :, :].unsqueeze(1).to_broadcast([128, 4, 2, 64])
                        snb = cs_[:, 1, :, :].unsqueeze(1).to_broadcast([128, 4, 2, 64])

                        def v4(ap):
                            return ap.rearrange("p (a b) d -> p a b d", b=2)
                        A_, b_A_ = tA.next()
                        B_, b_B_ = tB.next()
                        eng1 = "dve" if w_ == 0 else "pool"
                        kb.tt(v4(A_), v4(t1_), csb, ALU.mult, [b_qk, b_cs_], [b_A_], eng=eng1)
                        kb.tt(v4(B_), v4(t2_), snb, ALU.mult, [b_qk, b_cs_], [b_B_], eng=eng1)
                        kb.tt(o6[:, :, 0, :], A_, B_, ALU.subtract, [b_A_, b_B_], [b_qb], eng=eng1)
                        A2, b_A2 = tA.next()
                        B2, b_B2 = tB.next()
                        kb.tt(v4(A2), v4(t2_), csb, ALU.mult, [b_qk, b_cs_], [b_A2], eng=eng1)
                        kb.tt(v4(B2), v4(t1_), snb, ALU.mult, [b_qk, b_cs_], [b_B2], eng=eng1)
                        kb.tt(o6[:, :, 1, :], A2, B2, ALU.add, [b_A2, b_B2], [b_qb], eng=eng1)
                st[("qb", c)] = (qb, b_qb)

            def C2(c):
                rows = slice(c * 128, (c + 1) * 128)
                qb, b_qb = st.pop(("qb", c))
                kb.dma(KB_[rows, :], qb[:, 1, :], [b_qb], [b_KB[c]])
                for w_, (dst, bd) in enumerate(((QT, b_QT[c]), (KT, b_KT[c]))):
                    tp, b_tp = tpr.next()
                    for q in range(8):
                        kb.tr(tp[:, q, :], qb[:, w_, q * 128:(q + 1) * 128], [b_qb], [b_tp])
                    tT, b_tT = tTr.next()
                    kb.copy(tT, tp, [b_tp], [b_tT], eng="act" if w_ else "dve")
                    kb.dma(dst[c], tT, [b_tT], [bd])
            pipeline(list(range(NCH)), [N_, A, B, C, C2])
        P.barrier()
        with ExitStack() as es:
            pyr = kb.psring(es, [128, 512], F32, 2, "wpy")
            pir = kb.psring(es, [128, 512], F32, 2, "wpi")
            pstr = kb.psring(es, [128, 512], F32, 2, "wpst")
            consts = {}
            for sweep in ("b", "f"):
                fin = sweep == "f"
                M = msk["M" + sweep]
                sgn = 1.0 if fin else -1.0
                lg, b_lg = kb.sb(es, [128, 4], F32, "lg")
                kb.bcast_load(lg, b_lg, ret_decay[0, (0 if fin else 4):(4 if fin else 8)])
                kb.act(lg, lg, AF.Exp, [b_lg], [b_lg])
                kb.ts(lg, lg, -1.0, None, ALU.mult, None, [b_lg], [b_lg])
                LT, b_LT = kb.sb(es, [128, 4, 128], BF16, "LT")
                vec, b_vec = kb.sb(es, [128, 3, 4], F32, "vec")
                with ExitStack() as es2:
                    Di, b_Di = kb.sb(es2, [128, 128], I32, "Di")
                    kb.iota(Di, [[1, 128]], -1, [b_Di])
                    Df, b_Df = kb.sb(es2, [128, 128], F32, "Df")
                    kb.copy(Df, Di, [b_Di], [b_Df])
                    arg, b_arg = kb.sb(es2, [128, 128], F32, "arg")
                    lgs, b_lgs = kb.sb(es2, [128, 4], F32, "lgs")
                    kb.ts(lgs, lg, sgn, None, ALU.mult, None, [b_lg], [b_lgs])
                    for h in range(4):
                        kb.stt(arg, Df, lgs[:, h:h + 1], M[0], ALU.mult, ALU.add, [b_Df, b_lgs, M[1]], [b_arg])
                        kb.act(LT[:, h, :], arg, AF.Exp, [b_arg], [b_LT])
                    pi_, b_pi = kb.sb(es2, [128, 1], I32, "pi2")
                    kb.iota(pi_, [[0, 1]], 1, [b_pi])
                    pf, b_pf = kb.sb(es2, [128, 3], F32, "pf2")
                    kb.copy(pf[:, 0:1], pi_, [b_pi], [b_pf])
                    if fin:
                        kb.ts(pf[:, 1:2], pf[:, 0:1], 1.0, None, ALU.add, None, [b_pf], [b_pf])
                        kb.ts(pf[:, 2:3], pf[:, 0:1], -1.0, 127.0, ALU.mult, ALU.add, [b_pf], [b_pf])
                    else:
                        kb.ts(pf[:, 1:2], pf[:, 0:1], -1.0, 128.0, ALU.mult, ALU.add, [b_pf], [b_pf])
                        kb.copy(pf[:, 2:3], pf[:, 0:1], [b_pf], [b_pf])
                    kb.ts(vec[:, 0, :], lg, pf[:, 1:2], None, ALU.mult, None, [b_lg, b_pf], [b_vec])
                    kb.ts(vec[:, 1, :], lg, pf[:, 2:3], None, ALU.mult, None, [b_lg, b_pf], [b_vec])
                    kb.ts(vec[:, 2, :], lg, 128.0, None, ALU.mult, None, [b_lg], [b_vec])
                    kb.act(vec, vec, AF.Exp, [b_vec], [b_vec])
                    P.barrier()
                consts[sweep] = (LT, b_LT, vec, b_vec)

            def make_sweep(sweep):
                fin = sweep == "f"
                LT, b_LT, vec, b_vec = consts[sweep]
                YD, b_YD = (YF, b_YF) if fin else (YB, b_YB)
                S, b_S = kb.sb(es, [128, 8, 512], F32, "RS")
                Sb, b_Sb = kb.sb(es, [128, 8, 512], BF16, "RSb")
                kb.memset(S, 0.0, [b_S])
                kb.memset(Sb, 0.0, [b_Sb])
                qtr = kb.sbring(es, [128, 8, 128], BF16, 3, "wqt")
                ktr = kb.sbring(es, [128, 8, 128], BF16, 3, "wkt")
                kkr = kb.sbring(es, [128, 1024], BF16, 3, "wk")
                vr = kb.sbring(es, [128, 4, 512], BF16, 3, "wv")
                vtr = kb.sbring(es, [128, 4, 512], BF16, 2, "wvt")
                atr = kb.sbring(es, [128, 4, 128], BF16, 2, "wat")
                t1r = kb.sbring(es, [128, 512], F32, 2, "wt1")
                ydr = kb.sbring(es, [128, 2048], BF16, 2, "wyd")
                psc = kb.psring(es, [128, 4, 128], F32, 1, "wpsc")
                order = list(range(NCH)) if fin else [1, 0] + list(range(NCH - 1, 1, -1))
                st = {}

                def A(c):
                    rows = slice(c * 128, (c + 1) * 128)
                    qt, b_qt = qtr.next()
                    kb.dma(qt, QT[c], [b_QT[c]], [b_qt])
                    kt, b_kt = ktr.next()
                    kb.dma(kt, KT[c], [b_KT[c]], [b_kt])
                    kk, b_kk = kkr.next()
                    kb.dma(kk, KB_[rows, :], [b_KB[c]], [b_kk])
                    v, b_v = vr.next()
                    kb.dma(v, VV[rows, :].rearrange("p (h d) -> p h d", d=512), [b_VV[c]], [b_v])
                    st[c] = dict(qt=qt, b_qt=b_qt, kt=kt, b_kt=b_kt, kk=kk, b_kk=b_kk, v=v, b_v=b_v)
                    yield

                def B(c):
                    s_ = st[c]
                    qt, b_qt, kt, b_kt, v, b_v = s_["qt"], s_["b_qt"], s_["kt"], s_["b_kt"], s_["v"], s_["b_v"]
                    ps_, b_ps = psc.next()
                    for h in range(4):
                        for n_ in range(2):
                            kb.mm(ps_[:, h, :], kt[:, 2 * h + n_, :], qt[:, 2 * h + n_, :], [b_kt, b_qt], [b_ps], start=(n_ == 0), stop=(n_ == 1))
                    at, b_at = atr.next()
                    kb.tt(at, ps_, LT, ALU.mult, [b_ps, b_LT], [b_at])
                    vt, b_vt = vtr.next()
                    kb.tt(vt, v, vec[:, 1, :].unsqueeze(2).to_broadcast([128, 4, 512]), ALU.mult, [b_v, b_vec], [b_vt], eng="pool")
                    s_.update(at=at, b_at=b_at, vt=vt, b_vt=b_vt)
                    yield

                def C(c):
                    s_ = st.pop(c)
                    rows = slice(c * 128, (c + 1) * 128)
                    qt, b_qt, kk, b_kk, v, b_v = s_["qt"], s_["b_qt"], s_["kk"], s_["b_kk"], s_["v"], s_["b_v"]
                    at, b_at, vt, b_vt = s_["at"], s_["b_at"], s_["vt"], s_["b_vt"]
                    yd, b_yd = ydr.next()
                    for h in range(4):
                        py, b_py = pyr.next()
                        kb.mm(py, at[:, h, :], v[:, h, :], [b_at, b_v], [b_py])
                        pi2, b_pi2 = pir.next()
                        for n_ in range(2):
                            kb.mm(pi2, qt[:, 2 * h + n_, :], Sb[:, 2 * h + n_, :], [b_qt, b_Sb], [b_pi2], start=(n_ == 0), stop=(n_ == 1))
                        t1, b_t1 = t1r.next()
                        kb.act(t1, pi2, AF.Identity, [b_pi2, b_vec], [b_t1], scale=vec[:, 0, h:h + 1])
                        kb.tt(yd[:, h * 512:(h + 1) * 512], py, t1, ALU.add, [b_py, b_t1], [b_yd])
                        if h % 2 == 1:
                            yield
                    for h in range(4):
                        for n_ in range(2):
                            pt, b_pt = pstr.next()
                            kb.mm(pt, kk[:, h * 256 + n_ * 128:h * 256 + (n_ + 1) * 128], vt[:, h, :], [b_kk, b_vt], [b_pt])
                            Sg = S[:, 2 * h + n_, :]
                            kb.stt(Sg, Sg, vec[:, 2, h:h + 1], pt, ALU.mult, ALU.add, [b_S, b_vec, b_pt], [b_S])
                    kb.copy(Sb, S, [b_S], [b_Sb], eng="act")
                    kb.dma(YD[rows, :], yd, [b_yd], [b_YD[c]])
                    yield
                stages = [A, B, C]
                n, S_ = len(order), 3
                for step in range(n + S_ - 1):
                    for si in range(S_):
                        i = step - si
                        if 0 <= i < n:
                            yield from stages[si](order[i])
            gens = [make_sweep("b"), make_sweep("f")]
            while gens:
                for g in list(gens):
                    try:
                        next(g)
                    except StopIteration:
                        gens.remove(g)
        P.barrier()
        phase_finish("ret", 0, ret_w_out[0])

    def phase_final(raw=False):
        with ExitStack() as es:
            nr = norm_setup(es, tmp=False, nss=4)
            fg, b_fg = kb.sb(es, [128, D], F32, "fg")
            kb.bcast_load(fg, b_fg, final_g)
            xr = kb.sbring(es, [128, D], F32, 4, "zx")
            orr = kb.sbring(es, [128, D], F32, 3, "zo")
            st = {}

            def A(c):
                xc, b_xc = xr.next()
                kb.dma(xc, X[c * 128:(c + 1) * 128, :], [b_X[c]], [b_xc])
                st[c] = (xc, b_xc)

            def B(c):
                xc, b_xc = st.pop(c)
                if raw:
                    kb.dma(out[(c - 2) * 128:(c - 1) * 128, :], xc, [b_xc], [b_out[c]])
                    return
                ss, b_ss = rstd_of(nr, xc, b_xc, D)
                o, b_o = orr.next()
                kb.stt(o, xc, ss, fg, ALU.mult, ALU.mult, [b_xc, b_ss, b_fg], [b_o])
                kb.dma(out[(c - 2) * 128:(c - 1) * 128, :], o, [b_o], [b_out[c]])
            pipeline(list(range(2, NCH)), [A, lambda c: None, B])
        P.barrier()

    kinds = [0, 1, 2, 0]
    for L in range(debug_layers):
        phase_mod(L)
        if kinds[L] == 0:
            phase_ssd(L, L // 3)
        elif kinds[L] == 1:
            phase_gmlp(L)
        else:
            phase_ret(L)
        if not (skip_last_ffn and L == debug_layers - 1):
            phase_ffn(L)
    phase_final(raw=debug_raw)
    P.op("sp", None, reads=b_out.all())
    P.emit()
    glob.close()
    return nc


def make_in_maps(inputs, cores):
    f = lambda a: np.ascontiguousarray(np.asarray(a, dtype=np.float32))
    shared = {
        "mod_w": f(inputs["mod_w"]), "mod_b": f(inputs["mod_b"]), "norm1_g": f(inputs["norm1_g"]), "norm2_g": f(inputs["norm2_g"]),
        "ffn_w1": f(inputs["ffn_w1"]), "ffn_w3": f(inputs["ffn_w3"]), "ffn_w2": f(inputs["ffn_w2"]),
        "ssd_w_in": f(inputs["ssd_w_in"]), "ssd_conv_w": f(inputs["ssd_conv_w"]), "ssd_conv_b": f(inputs["ssd_conv_b"]),
        "ssd_alog": f(np.concatenate([inputs["ssd_a_log_f"], inputs["ssd_a_log_b"]], axis=1)),
        "ssd_dtb": f(np.concatenate([inputs["ssd_dt_bias_f"], inputs["ssd_dt_bias_b"]], axis=1)),
        "ssd_d": f(inputs["ssd_d"]), "ssd_norm_g": f(inputs["ssd_norm_g"]), "ssd_w_out": f(inputs["ssd_w_out"]),
        "gmlp_w_in": f(inputs["gmlp_w_in"]), "gmlp_norm_g": f(inputs["gmlp_norm_g"]), "gmlp_w_s": f(inputs["gmlp_w_s"]),
        "gmlp_b_s": f(inputs["gmlp_b_s"]), "gmlp_w_out": f(inputs["gmlp_w_out"]),
        "ret_w_in": f(inputs["ret_w_in"]),
        "ret_decay": f(np.concatenate([inputs["ret_decay_f"], inputs["ret_decay_b"]], axis=1)),
        "ret_w_out": f(inputs["ret_w_out"]), "final_g": f(inputs["final_g"]),
    }
    maps = []
    for b in cores:
        m = dict(shared)
        m["xin"] = f(np.concatenate([inputs["ctx"][b], inputs["x"][b]], axis=0))
        m["cc"] = f(np.stack([inputs["c"][b], inputs["c_ctx"]], axis=0))
        maps.append(m)
    return maps


def kernel(**inputs):
    nc = build_program()
    B = inputs["x"].shape[0]
    maps = make_in_maps(inputs, list(range(B)))
    res = run_bass_kernel_spmd(nc, maps, core_ids=list(range(B)))
    return np.stack([np.asarray(r["out"], dtype=np.float32) for r in res.results], axis=0)
```
